# Optimizing a Trainium2 kernel written in Bass

```python
import math
import jax, jax.numpy as jnp
from jax import lax
import numpy as np

D_MODEL = 1024
BATCH = 8
SEQ = 4096
DEPTH = 4

CHUNK = 64
Q_BLOCK = 128
EPS = 1e-6
CONV_WIDTH = D_MODEL
CONV_KERNEL = 31
GLA_HEADS = 4
GLA_DK = D_MODEL // 2 // GLA_HEADS
GLA_DV = D_MODEL // GLA_HEADS
GLA_RANK = 16
GLA_GATE_NORM = 16.0
DIFF_HEADS = 8
DIFF_HD = D_MODEL // DIFF_HEADS // 2
DIFF_VD = 2 * DIFF_HD
FFN_HIDDEN = -(-8 * D_MODEL // (3 * 256)) * 256

IN_SPLITS = (
    CONV_WIDTH, CONV_WIDTH,
    GLA_HEADS * GLA_DK, GLA_HEADS * GLA_DK,
    GLA_HEADS * GLA_DV, GLA_HEADS * GLA_DV,
    GLA_RANK,
    DIFF_HEADS * 2 * DIFF_HD, DIFF_HEADS * 2 * DIFF_HD,
    DIFF_HEADS * DIFF_VD,
    D_MODEL, D_MODEL, D_MODEL,
)
IN_OFFSETS = tuple(sum(IN_SPLITS[:i + 1]) for i in range(len(IN_SPLITS) - 1))
IN_WIDTH = sum(IN_SPLITS)

kernel_name = 'hybrid_conv_gla_diffattn_streaming'


def rms_norm(x, g):
    xf = x.astype(jnp.float32)
    y = xf * lax.rsqrt(jnp.mean(xf * xf, axis=-1, keepdims=True) + EPS)
    return (y * g.astype(jnp.float32)).astype(x.dtype)


def layer_norm(x, g, b):
    xf = x.astype(jnp.float32)
    mu = jnp.mean(xf, axis=-1, keepdims=True)
    xc = xf - mu
    y = xc * lax.rsqrt(jnp.mean(xc * xc, axis=-1, keepdims=True) + EPS)
    return (y * g.astype(jnp.float32) + b.astype(jnp.float32)).astype(x.dtype)


def conv_module(a, gate, w_dw, ln_g, ln_b, w_pw):
    u = a * jax.nn.sigmoid(gate)
    rhs = w_dw[:, None, :].astype(u.dtype)
    u = lax.conv_general_dilated(
        u, rhs, window_strides=(1,), padding=[(CONV_KERNEL - 1, 0)],
        dimension_numbers=('NWC', 'WIO', 'NWC'), feature_group_count=CONV_WIDTH)
    u = jax.nn.silu(layer_norm(u, ln_g, ln_b))
    return u @ w_pw


def gla(q, k, v, og, f_low, w_f_up, b_f, norm_g):
    bsz, seq, _ = q.shape
    n = seq // CHUNK
    dt = v.dtype
    f32 = jnp.float32
    q = q.reshape(bsz, n, CHUNK, GLA_HEADS, GLA_DK).astype(f32) * (GLA_DK ** -0.5)
    k = k.reshape(bsz, n, CHUNK, GLA_HEADS, GLA_DK).astype(f32)
    v = v.reshape(bsz, n, CHUNK, GLA_HEADS, GLA_DV).astype(f32)
    logf = jax.nn.log_sigmoid((f_low @ w_f_up + b_f).astype(f32)) / GLA_GATE_NORM
    g = jnp.cumsum(logf.reshape(bsz, n, CHUNK, GLA_HEADS, GLA_DK), axis=2)
    g_last = g[:, :, -1:]
    q_g = q * jnp.exp(g)
    k_g = k * jnp.exp(-g)
    k_d = k * jnp.exp(g_last - g)
    causal = jnp.tril(jnp.ones((CHUNK, CHUNK), dtype=bool))
    att = jnp.einsum('bnihd,bnjhd->bnhij', q_g, k_g)
    att = jnp.where(causal, att, 0.0)
    o_intra = jnp.einsum('bnhij,bnjhv->bnihv', att, v)
    u = jnp.einsum('bnjhd,bnjhv->nbhdv', k_d, v)
    decay = jnp.exp(g_last[:, :, 0]).transpose(1, 0, 2, 3)

    def step(s, inp):
        dec, un = inp
        return dec[..., None] * s + un, s

    s0 = jnp.zeros((bsz, GLA_HEADS, GLA_DK, GLA_DV), f32)
    _, s_prev = lax.scan(step, s0, (decay, u))
    o_inter = jnp.einsum('bnihd,nbhdv->bnihv', q_g, s_prev)
    o = rms_norm(o_intra + o_inter, norm_g)
    o = o.reshape(bsz, seq, GLA_HEADS * GLA_DV).astype(dt)
    return o * jax.nn.silu(og)


def diff_attention(q, k, v, lq1, lk1, lq2, lk2, norm_g, lambda_init):
    bsz, seq, _ = q.shape
    f32 = jnp.float32
    q = q.reshape(bsz, seq, DIFF_HEADS, 2, DIFF_HD)
    k = k.reshape(bsz, seq, DIFF_HEADS, 2, DIFF_HD)
    v = v.reshape(bsz, seq, DIFF_HEADS, DIFF_VD)
    lam = (jnp.exp(jnp.sum(lq1.astype(f32) * lk1.astype(f32)))
           - jnp.exp(jnp.sum(lq2.astype(f32) * lk2.astype(f32))) + lambda_init)
    nqb = seq // Q_BLOCK
    qb = q.reshape(bsz, nqb, Q_BLOCK, DIFF_HEADS, 2, DIFF_HD).transpose(1, 0, 2, 3, 4, 5)
    key_chunk = jnp.arange(seq) // CHUNK
    scale = DIFF_HD ** -0.5

    def block(args):
        qi, idx = args
        s = jnp.einsum('bqhtd,bkhtd->bhtqk', qi, k, preferred_element_type=f32) * scale
        q_chunk = (idx * Q_BLOCK + jnp.arange(Q_BLOCK)) // CHUNK
        mask = key_chunk[None, :] <= q_chunk[:, None]
        s = jnp.where(mask, s, -jnp.inf)
        p = jax.nn.softmax(s, axis=-1)
        pd = p[:, :, 0] - lam * p[:, :, 1]
        return jnp.einsum('bhqk,bkhv->bqhv', pd.astype(v.dtype), v)

    o = lax.map(block, (qb, jnp.arange(nqb)))
    o = o.transpose(1, 0, 2, 3, 4).reshape(bsz, seq, DIFF_HEADS, DIFF_VD)
    o = rms_norm(o, norm_g) * (1.0 - lambda_init)
    return o.reshape(bsz, seq, DIFF_HEADS * DIFF_VD)


def setup_inputs(seed: int = 0) -> dict:
    key = jax.random.key(seed)
    ks = jax.random.split(key, 20)
    f32 = jnp.float32
    L, D = DEPTH, D_MODEL

    def nrm(k, shape, scale):
        return jax.random.normal(k, shape, f32) * scale

    return {
        'x': nrm(ks[0], (BATCH, SEQ, D), 1.0),
        'norm_mix_g': 1.0 + nrm(ks[1], (L, D), 0.02),
        'w_in': nrm(ks[2], (L, D, IN_WIDTH), D ** -0.5),
        'conv_dw': nrm(ks[3], (L, CONV_KERNEL, CONV_WIDTH), CONV_KERNEL ** -0.5),
        'conv_ln_g': 1.0 + nrm(ks[4], (L, CONV_WIDTH), 0.02),
        'conv_ln_b': nrm(ks[5], (L, CONV_WIDTH), 0.02),
        'w_conv_out': nrm(ks[6], (L, CONV_WIDTH, D), CONV_WIDTH ** -0.5),
        'gla_wf_up': nrm(ks[7], (L, GLA_RANK, GLA_HEADS * GLA_DK), GLA_RANK ** -0.5),
        'gla_bf': nrm(ks[8], (L, GLA_HEADS * GLA_DK), 0.1),
        'gla_norm_g': 1.0 + nrm(ks[9], (L, GLA_DV), 0.02),
        'diff_lq1': nrm(ks[10], (L, DIFF_HD), 0.1),
        'diff_lk1': nrm(ks[11], (L, DIFF_HD), 0.1),
        'diff_lq2': nrm(ks[12], (L, DIFF_HD), 0.1),
        'diff_lk2': nrm(ks[13], (L, DIFF_HD), 0.1),
        'diff_norm_g': 1.0 + nrm(ks[14], (L, DIFF_VD), 0.02),
        'w_out': nrm(ks[15], (L, D, D), D ** -0.5),
        'norm_ffn_g': 1.0 + nrm(ks[16], (L, D), 0.02),
        'w_ffn_in': nrm(ks[17], (L, D, 2 * FFN_HIDDEN), D ** -0.5),
        'w_ffn_out': nrm(ks[18], (L, FFN_HIDDEN, D), FFN_HIDDEN ** -0.5),
        'final_norm_g': 1.0 + nrm(ks[19], (D,), 0.02),
    }


def reference(x, norm_mix_g, w_in, conv_dw, conv_ln_g, conv_ln_b, w_conv_out,
              gla_wf_up, gla_bf, gla_norm_g, diff_lq1, diff_lk1, diff_lq2, diff_lk2,
              diff_norm_g, w_out, norm_ffn_g, w_ffn_in, w_ffn_out, final_norm_g):
    for l in range(DEPTH):
        lambda_init = 0.8 - 0.6 * math.exp(-0.3 * l)
        h = rms_norm(x, norm_mix_g[l])
        proj = h @ w_in[l]
        (c_val, c_gate, g_q, g_k, g_v, g_og, g_f, d_q, d_k, d_v,
         m_conv, m_gla, m_diff) = jnp.split(proj, IN_OFFSETS, axis=-1)
        o_conv = conv_module(c_val, c_gate, conv_dw[l], conv_ln_g[l], conv_ln_b[l], w_conv_out[l])
        o_gla = gla(g_q, g_k, g_v, g_og, g_f, gla_wf_up[l], gla_bf[l], gla_norm_g[l])
        o_diff = diff_attention(d_q, d_k, d_v, diff_lq1[l], diff_lk1[l], diff_lq2[l], diff_lk2[l],
                                diff_norm_g[l], lambda_init)
        y = (jax.nn.sigmoid(m_conv) * o_conv + jax.nn.sigmoid(m_gla) * o_gla
             + jax.nn.sigmoid(m_diff) * o_diff)
        x = x + y @ w_out[l]
        h = rms_norm(x, norm_ffn_g[l])
        gate, up = jnp.split(h @ w_ffn_in[l], 2, axis=-1)
        x = x + (jax.nn.silu(gate) * up) @ w_ffn_out[l]
    return rms_norm(x, final_norm_g)
```

```python
import numpy as np
import concourse.bass as bass
import concourse.mybir as mybir
from concourse.bass_utils import run_bass_kernel_spmd
from contextlib import ExitStack

F32 = mybir.dt.float32
BF16 = mybir.dt.bfloat16
AF = mybir.ActivationFunctionType
ALU = mybir.AluOpType

SAME_ENGINE_SYNC = True


class Buf:
    __slots__ = ('w', 'r')

    def __init__(self):
        self.w = None
        self.r = {}


class FW:
    def __init__(self, nc):
        self.nc = nc
        self.eng = {'pe': nc.tensor, 'dve': nc.vector, 'act': nc.scalar, 'pool': nc.gpsimd, 'sp': nc.sync}
        self.stack = ExitStack()
        self.sem = {}
        self.cnt = {}
        self.waited = {e: {} for e in self.eng}
        for k in self.eng:
            self._mksem(k)
        self.nops = 0
        self.chan_n = {}
        self.CHAN_SLOTS = {'ld': 24, 'st': 24, 'wld': 8, 'cst': 4, 'out': 8}

    def _mksem(self, k):
        self.sem[k] = self.stack.enter_context(self.nc.semaphore("s_" + k))
        self.cnt[k] = 0

    def op(self, e, fn, reads=(), writes=(), sig=True, chan=None):
        waits = {}
        if chan:
            i = self.chan_n.get(chan, 0)
            self.chan_n[chan] = i + 1
            key = "%s_%d" % (chan, i % self.CHAN_SLOTS.get(chan, 8))
            if key not in self.sem:
                self._mksem(key)
            if self.cnt[key] > 0:
                waits[key] = self.cnt[key]
        else:
            key = e
        inc = 16 if chan else 1

        def need(tok):
            if tok is None:
                return
            k, v = tok
            if k == e and not chan and (e == 'pe' or not SAME_ENGINE_SYNC):
                return
            if waits.get(k, 0) < v:
                waits[k] = v

        for b in reads:
            need(b.w)
        for b in writes:
            need(b.w)
            for t in b.r.items():
                need(t)
        eng = self.eng[e]
        wd = self.waited[e]
        for k, v in waits.items():
            if wd.get(k, 0) >= v:
                continue
            eng.wait_ge(self.sem[k], v)
            wd[k] = v
        ins = fn(eng)
        self.nops += 1
        if sig:
            self.cnt[key] += inc
            ins.then_inc(self.sem[key], inc)
            tok = (key, self.cnt[key])
        else:
            tok = (key, self.cnt[key] + inc)
        for b in reads:
            if b.r.get(tok[0], 0) < tok[1]:
                b.r[tok[0]] = tok[1]
        for b in writes:
            b.w = tok
            b.r = {}
        return ins

    def barrier(self, engines=None):
        for e in (engines or self.eng):
            eng = self.eng[e]
            wd = self.waited[e]
            for k, v in self.cnt.items():
                if v > 0 and wd.get(k, 0) < v:
                    eng.wait_ge(self.sem[k], v)
                    wd[k] = v

    def close(self):
        self.stack.close()


D = 1024
KC = 8
INW = 11280
FH = 2816
EPS = 1e-6
O_CVAL, O_CGATE, O_GQ, O_GK, O_GV, O_GOG, O_GF = 0, 1024, 2048, 2560, 3072, 4096, 5120
O_DQ, O_DK, O_DV, O_MC, O_MG, O_MD = 5136, 6160, 7184, 8208, 9232, 10256
import math


def build_program(S=4096, L=4, dbg=False, stop_after=None):
    NT = S // 512
    NB = S // 128
    nc = bass.Bass("TRN2", target_bir_lowering=False)

    def din(name, shape):
        return nc.dram_tensor(name, shape, F32, kind="ExternalInput").ap()

    x_in = din("x", [S, D])
    norm_mix_g = din("norm_mix_g", [L, D])
    w_in = din("w_in", [L, D, INW])
    conv_dw = din("conv_dw", [L, 31, D])
    conv_ln_g = din("conv_ln_g", [L, D])
    conv_ln_b = din("conv_ln_b", [L, D])
    w_conv_out = din("w_conv_out", [L, D, D])
    gla_wf_up = din("gla_wf_up", [L, 16, 512])
    gla_bf = din("gla_bf", [L, 512])
    gla_norm_g = din("gla_norm_g", [L, 256])
    diff_lq1 = din("diff_lq1", [L, 64])
    diff_lk1 = din("diff_lk1", [L, 64])
    diff_lq2 = din("diff_lq2", [L, 64])
    diff_lk2 = din("diff_lk2", [L, 64])
    diff_norm_g = din("diff_norm_g", [L, 128])
    w_out = din("w_out", [L, D, D])
    norm_ffn_g = din("norm_ffn_g", [L, D])
    w_ffn_in = din("w_ffn_in", [L, D, 2 * FH])
    w_ffn_out = din("w_ffn_out", [L, FH, D])
    final_norm_g = din("final_norm_g", [D])
    out = nc.dram_tensor("out", [S, D], F32, kind="ExternalOutput").ap()

    skind = "ExternalOutput" if dbg else "Internal"

    def dscr(name, shape, dt):
        return nc.dram_tensor(name, shape, dt, kind=skind).ap()

    xT_d = dscr("s_xT", [D, S], F32)
    cv_d = dscr("s_cv", [D, S], BF16)
    gq_d = dscr("s_gq", [512, S], BF16)
    gk_d = dscr("s_gk", [512, S], BF16)
    gf_d = dscr("s_gf", [16, S], BF16)
    gg_d = dscr("s_gg", [D, S], BF16)
    dq_d = dscr("s_dq", [D, S], BF16)
    dk_d = dscr("s_dk", [D, S], BF16)
    mc_d = dscr("s_mc", [D, S], BF16)
    md_d = dscr("s_md", [D, S], BF16)
    gv_d = dscr("s_gv", [S, D], BF16)
    dv_d = dscr("s_dv", [S, D], BF16)
    gkt_d = dscr("s_gkt", [S, 512], BF16)
    yg_d = dscr("s_yg", [D, S], BF16)
    yd_d = dscr("s_yd", [D, S], BF16)
    aT_d = dscr("s_aT", [FH, S], BF16)

    fw = FW(nc)
    op = fw.op
    top = ExitStack()

    uid = [0]

    def alloc(st, name, shape, dt):
        uid[0] += 1
        return st.enter_context(nc.sbuf_tensor("%s_u%d" % (name, uid[0]), shape, dt))

    ps = [top.enter_context(nc.psum_tensor("ps%d" % i, [128, 512], F32)) for i in range(8)]
    bps = [Buf() for _ in range(8)]
    hT = alloc(top, "hT", [128, KC, S], BF16)
    bhT = [Buf() for _ in range(NT)]
    ident = alloc(top, "ident", [128, 128], F32)
    ident_bf = alloc(top, "ident_bf", [128, 128], BF16)
    ones_bf = alloc(top, "ones_bf", [128, 128], BF16)
    b_const = Buf()
    g_mix = alloc(top, "g_mix", [128, L, KC], F32)
    g_ffn = alloc(top, "g_ffn", [128, L, KC], F32)
    g_fin = alloc(top, "g_fin", [128, KC], F32)
    cln_g = alloc(top, "cln_g", [128, L, KC], F32)
    cln_b = alloc(top, "cln_b", [128, L, KC], F32)
    nbf = alloc(top, "nbf", [128, L, 4], F32)
    gng = alloc(top, "gng", [128, L, 2], F32)
    dng = alloc(top, "dng", [128, L], F32)
    nlam = alloc(top, "nlam", [128, L], F32)
    lqk = alloc(top, "lqk", [128, 4, L, 64], F32)
    lsum = alloc(top, "lsum", [128, 2, L], F32)
    lprod = alloc(top, "lprod", [128, 2, L, 64], F32)
    eps_t = alloc(top, "eps_t", [128, 1], F32)
    one_t = alloc(top, "one_t", [128, 1], F32)
    lnsc_t = alloc(top, "lnsc_t", [128, 1], F32)

    pool_q = 'pool'

    with nc.allow_non_contiguous_dma(reason="tiny per-layer parameter vectors"):
        op('pool', lambda e: e.memset(ident[:], 1.0), writes=[b_const])
        op('pool', lambda e: e.affine_select(out=ident[:], in_=ident[:], pattern=[[-1, 128]], base=0,
                                             channel_multiplier=1, compare_op=ALU.is_equal, fill=0.0),
           reads=[b_const], writes=[b_const])
        op('pool', lambda e: e.tensor_copy(out=ident_bf[:], in_=ident[:]), reads=[b_const], writes=[b_const])
        op('pool', lambda e: e.memset(ones_bf[:], 1.0), writes=[b_const])
        op('pool', lambda e: e.memset(eps_t[:], EPS), writes=[b_const])
        op('pool', lambda e: e.memset(one_t[:], 1.0), writes=[b_const])
        op('pool', lambda e: e.memset(lnsc_t[:], math.log(128.0 ** -0.5)), writes=[b_const])
        for l in range(L):
            op('sp', lambda e: e.dma_start(out=g_mix[:, l, :], in_=norm_mix_g[l].rearrange("(c p) -> p c", p=128)), writes=[b_const], chan='cst')
            op('sp', lambda e: e.dma_start(out=g_ffn[:, l, :], in_=norm_ffn_g[l].rearrange("(c p) -> p c", p=128)), writes=[b_const], chan='cst')
            op('sp', lambda e: e.dma_start(out=cln_g[:, l, :], in_=conv_ln_g[l].rearrange("(c p) -> p c", p=128)), writes=[b_const], chan='cst')
            op('sp', lambda e: e.dma_start(out=cln_b[:, l, :], in_=conv_ln_b[l].rearrange("(c p) -> p c", p=128)), writes=[b_const], chan='cst')
            op('sp', lambda e: e.dma_start(out=nbf[:, l, :], in_=gla_bf[l].rearrange("(c p) -> p c", p=128)), writes=[b_const], chan='cst')
            op('sp', lambda e: e.dma_start(out=gng[:, l, :], in_=gla_norm_g[l].rearrange("(c p) -> p c", p=128)), writes=[b_const], chan='cst')
            op('sp', lambda e: e.dma_start(out=dng[:, l:l + 1], in_=diff_norm_g[l].rearrange("(c p) -> p c", p=128)), writes=[b_const], chan='cst')
            for i, t in enumerate([diff_lq1, diff_lk1, diff_lq2, diff_lk2]):
                op('sp', lambda e: e.dma_start(out=lqk[:, i, l, :], in_=t[l:l + 1, :].broadcast_to([128, 64])), writes=[b_const], chan='cst')
        op('sp', lambda e: e.dma_start(out=g_fin[:], in_=final_norm_g.rearrange("(c p) -> p c", p=128)), writes=[b_const], chan='cst')
        op('dve', lambda e: e.tensor_scalar(out=nbf[:], in0=nbf[:], scalar1=-1.0, scalar2=None, op0=ALU.mult), reads=[b_const], writes=[b_const])
        for j in range(2):
            op('dve', lambda e: e.tensor_tensor(out=lprod[:, j], in0=lqk[:, 2 * j], in1=lqk[:, 2 * j + 1], op=ALU.mult), reads=[b_const], writes=[b_const])
            op('dve', lambda e: e.tensor_reduce(out=lsum[:, j, :], in_=lprod[:, j], axis=mybir.AxisListType.X, op=ALU.add), reads=[b_const], writes=[b_const])
        op('act', lambda e: e.activation(out=lsum[:], in_=lsum[:], func=AF.Exp), reads=[b_const], writes=[b_const])
        op('dve', lambda e: e.tensor_tensor(out=nlam[:], in0=lsum[:, 1, :], in1=lsum[:, 0, :], op=ALU.subtract), reads=[b_const], writes=[b_const])
        for l in range(L):
            li = 0.8 - 0.6 * math.exp(-0.3 * l)
            op('dve', lambda e: e.tensor_scalar(out=nlam[:, l:l + 1], in0=nlam[:, l:l + 1], scalar1=-li, scalar2=None, op0=ALU.add), reads=[b_const], writes=[b_const])
            op('dve', lambda e: e.tensor_scalar(out=dng[:, l:l + 1], in0=dng[:, l:l + 1], scalar1=1.0 - li, scalar2=None, op0=ALU.mult), reads=[b_const], writes=[b_const])

    psrr = [0]

    def next_ps():
        i = psrr[0] % 8
        psrr[0] += 1
        return i

    def rmsnorm_tile(st_bufs, xt, b_xt, g_ap, tt, fin_out=None, b_fin=None):
        sq, b_sq, sd, b_sd = st_bufs
        op('act', lambda e: e.activation(out=sq[:], in_=xt[:], func=AF.Square), reads=[b_xt], writes=[b_sq])
        pi = next_ps()
        for c in range(KC):
            op('pe', lambda e: e.matmul(ps[pi][:, :], lhsT=ones_bf[:], rhs=sq[:, c, :], start=(c == 0), stop=(c == KC - 1)),
               reads=[b_sq, b_const], writes=[bps[pi]], sig=(c == KC - 1))
        op('act', lambda e: e.activation(out=sd[:], in_=ps[pi][:, :], func=AF.Sqrt, bias=eps_t[:], scale=1.0 / D),
           reads=[bps[pi], b_const], writes=[b_sd])
        op('dve', lambda e: e.reciprocal(out=sd[:], in_=sd[:]), reads=[b_sd], writes=[b_sd])
        for c in range(KC):
            if fin_out is None:
                op('dve', lambda e: e.scalar_tensor_tensor(out=hT[:, c, tt * 512:(tt + 1) * 512], in0=xt[:, c, :], scalar=g_ap[:, c:c + 1],
                                                           in1=sd[:], op0=ALU.mult, op1=ALU.mult),
                   reads=[b_xt, b_sd, b_const], writes=[bhT[tt]])
            else:
                op('dve', lambda e: e.scalar_tensor_tensor(out=fin_out[:, c, :], in0=xt[:, c, :], scalar=g_ap[:, c:c + 1],
                                                           in1=sd[:], op0=ALU.mult, op1=ALU.mult),
                   reads=[b_xt, b_sd, b_const], writes=[b_fin])

    bx_d = [Buf() for _ in range(NT)]

    def phase0():
        with ExitStack() as st:
            xin = [alloc(st, "p0_xin%d" % i, [128, D], F32) for i in range(2)]
            bxin = [Buf() for _ in range(2)]
            xt = [alloc(st, "p0_xt%d" % i, [128, KC, 512], F32) for i in range(2)]
            bxt = [Buf() for _ in range(2)]
            sq = alloc(st, "p0_sq", [128, KC, 512], BF16)
            sd = alloc(st, "p0_sd", [128, 512], F32)
            nb = (Buf(), Buf())
            for tt in range(NT):
                xs = xt[tt % 2]
                bxs = bxt[tt % 2]
                for j in range(4):
                    tb = tt * 4 + j
                    xi = xin[tb % 2]
                    bxi = bxin[tb % 2]
                    op('sp', lambda e: e.dma_start(out=xi[:], in_=x_in[tb * 128:(tb + 1) * 128, :]), writes=[bxi], chan='ld')
                    for half in range(2):
                        pi = next_ps()
                        for q in range(4):
                            c = half * 4 + q
                            op('pe', lambda e: e.transpose(ps[pi][:, q * 128:(q + 1) * 128], xi[:, c * 128:(c + 1) * 128], ident[:]),
                               reads=[bxi, b_const], writes=[bps[pi]], sig=(q == 3))
                        eng = 'dve' if half == 0 else 'act'
                        if eng == 'dve':
                            op('dve', lambda e: e.tensor_copy(out=xs[:, half * 4:(half + 1) * 4, j * 128:(j + 1) * 128],
                                                              in_=ps[pi][:, :].rearrange("p (a b) -> p a b", a=4)),
                               reads=[bps[pi]], writes=[bxs])
                        else:
                            op('act', lambda e: e.activation(out=xs[:, half * 4:(half + 1) * 4, j * 128:(j + 1) * 128],
                                                             in_=ps[pi][:, :].rearrange("p (a b) -> p a b", a=4), func=AF.Copy),
                               reads=[bps[pi]], writes=[bxs])
                op('sp', lambda e: e.dma_start(out=xT_d[:, tt * 512:(tt + 1) * 512].rearrange("(c p) t -> p c t", p=128), in_=xs[:]),
                   reads=[bxs], writes=[bx_d[tt]], chan='st')
                rmsnorm_tile((sq, nb[0], sd, nb[1]), xs, bxs, g_mix[:, 0, :], tt)
            fw.barrier()

    phase0()
    if dbg:
        hT_dbg = nc.dram_tensor("dbg_hT", [128, KC, S], BF16, kind="ExternalOutput").ap()
        dbg_act = nc.dram_tensor("dbg_act", [128, KC, S], BF16, kind="ExternalOutput").ap()
        dbg_y = nc.dram_tensor("dbg_y", [128, KC, S], BF16, kind="ExternalOutput").ap()
        dbg_rstd = nc.dram_tensor("dbg_rstd", [128, S], F32, kind="ExternalOutput").ap()
        dbg_mean = nc.dram_tensor("dbg_mean", [128, S], F32, kind="ExternalOutput").ap()

        def dump_hT():
            op('sp', lambda e: e.dma_start(out=hT_dbg[:, :, :], in_=hT[:]), reads=bhT, writes=[Buf()], chan='st')
            fw.barrier()
        if stop_after == 'p0':
            dump_hT()
    if stop_after == 'p0':
        fw.barrier()
        fw.close()
        top.close()
        return nc

    def fm_resources(st, pfx):
        r = {}
        r['wst'] = [alloc(st, pfx + "_wst%d" % i, [128, KC, 128], F32) for i in range(4)]
        r['bwst'] = [Buf() for _ in range(4)]
        r['wbf'] = [alloc(st, pfx + "_wbf%d" % i, [128, KC, 128], BF16) for i in range(4)]
        r['bwbf'] = [Buf() for _ in range(4)]
        r['ot'] = [alloc(st, pfx + "_ot%d" % i, [128, 512], BF16) for i in range(4)]
        r['bot'] = [Buf() for _ in range(4)]
        r['tmp'] = [alloc(st, pfx + "_tmp%d" % i, [128, 512], F32) for i in range(4)]
        r['btmp'] = [Buf() for _ in range(4)]
        r['wi'] = 0
        r['oi'] = 0
        r['ti'] = 0
        return r

    def fm_run(r, Wl, jobs):
        def load(job):
            slots = []
            for (c0, n) in job[0]:
                i = r['wi'] % 4
                r['wi'] += 1
                op('sp', lambda e: e.dma_start(out=r['wst'][i][:, :, 0:n], in_=Wl[:, c0:c0 + n].rearrange("(kc p) f -> p kc f", p=128)),
                   writes=[r['bwst'][i]], chan='wld')
                op('pool', lambda e: e.tensor_copy(out=r['wbf'][i][:, :, 0:n], in_=r['wst'][i][:, :, 0:n]),
                   reads=[r['bwst'][i]], writes=[r['bwbf'][i]])
                slots.append(i)
            return slots
        nxt = load(jobs[0]) if jobs else None
        for ji, job in enumerate(jobs):
            slots = nxt
            if ji + 1 < len(jobs):
                nxt = load(jobs[ji + 1])
            for tt in range(NT):
                pis = []
                for gi, (c0, n) in enumerate(job[0]):
                    pi = next_ps()
                    pis.append(pi)
                    wb = r['wbf'][slots[gi]]
                    for kc in range(KC):
                        op('pe', lambda e: e.matmul(ps[pi][0:n, :], lhsT=wb[:, kc, 0:n], rhs=hT[:, kc, tt * 512:(tt + 1) * 512],
                                                    start=(kc == 0), stop=(kc == KC - 1)),
                           reads=[r['bwbf'][slots[gi]], bhT[tt]], writes=[bps[pi]], sig=(kc == KC - 1))
                job[1](tt, pis)
            if job[2] is not None:
                job[2]()

    def get_ot(r):
        i = r['oi'] % 4
        r['oi'] += 1
        return r['ot'][i], r['bot'][i]

    def get_tmp(r):
        i = r['ti'] % 4
        r['ti'] += 1
        return r['tmp'][i], r['btmp'][i]

    evrr = [0]

    def evac_copy(out_ap, in_ap, reads, writes):
        evrr[0] += 1
        if evrr[0] % 2 == 0:
            op('dve', lambda e: e.tensor_copy(out=out_ap, in_=in_ap), reads=reads, writes=writes)
        else:
            op('act', lambda e: e.activation(out=out_ap, in_=in_ap, func=AF.Copy), reads=reads, writes=writes)

    def store_fm(r, dst, row0, n, tt, o, bo, bdst):
        op('sp', lambda e: e.dma_start(out=dst[row0:row0 + n, tt * 512:(tt + 1) * 512], in_=o[0:n, :]), reads=[bo], writes=[bdst], chan='st')

    bd = {k: Buf() for k in ['cv', 'gq', 'gk', 'gf', 'gg', 'dq', 'dk', 'mc', 'md', 'gv', 'dv', 'gkt', 'yg', 'yd', 'aT']}

    def phaseP(l):
        Wl = w_in[l]
        with ExitStack() as st:
            r = fm_resources(st, "pp")
            ubuf = [alloc(st, "pp_u%d" % i, [128, 32 + S], BF16) for i in range(2)]
            bub = [Buf() for _ in range(2)]
            dg = [alloc(st, "pp_dg%d" % i, [128, 31, 128], BF16) for i in range(2)]
            bdg = [Buf() for _ in range(2)]
            wdw = alloc(st, "pp_wdw", [128, KC, 31], F32)
            bwdw = Buf()
            tms = alloc(st, "pp_tms", [128, KC, 512], F32)
            btms = Buf()
            tmb = [alloc(st, "pp_tmb%d" % i, [128, KC, 512], BF16) for i in range(2)]
            btmb = [Buf() for _ in range(2)]
            tmo = [alloc(st, "pp_tmo%d" % i, [128, 512], BF16) for i in range(4)]
            btmo = [Buf() for _ in range(4)]
            with nc.allow_non_contiguous_dma(reason="depthwise conv taps, 127KB once per layer"):
                for c in range(KC):
                    op('sp', lambda e: e.dma_start(out=wdw[:, c, :], in_=conv_dw[l][:, c * 128:(c + 1) * 128].rearrange("k p -> p k")),
                       writes=[bwdw], chan='cst')
            for i in range(2):
                op('pool', lambda e: e.memset(ubuf[i][:, 0:32], 0.0), writes=[bub[i]])
            jobs = []

            def mk_conv(i):
                def evac(tt, pis):
                    t, bt = get_tmp(r)
                    op('act', lambda e: e.activation(out=t[:], in_=ps[pis[0]][:, :], func=AF.Sigmoid), reads=[bps[pis[0]]], writes=[bt])
                    op('dve', lambda e: e.tensor_tensor(out=ubuf[i % 2][:, 32 + tt * 512:32 + (tt + 1) * 512], in0=ps[pis[1]][:, :], in1=t[:], op=ALU.mult),
                       reads=[bps[pis[1]], bt], writes=[bub[i % 2]])

                def post():
                    d = dg[i % 2]
                    for k in range(31):
                        op('dve', lambda e: e.tensor_scalar(out=d[:, k, :], in0=ident_bf[:], scalar1=wdw[:, i, k:k + 1], scalar2=None, op0=ALU.mult),
                           reads=[b_const, bwdw], writes=[bdg[i % 2]])
                    for tt in range(NT):
                        pi = next_ps()
                        for k in range(31):
                            op('pe', lambda e: e.matmul(ps[pi][:, :], lhsT=d[:, k, :], rhs=ubuf[i % 2][:, 2 + k + tt * 512:2 + k + (tt + 1) * 512],
                                                        start=(k == 0), stop=(k == 30)),
                               reads=[bdg[i % 2], bub[i % 2]], writes=[bps[pi]], sig=(k == 30))
                        o, bo = get_ot(r)
                        evac_copy(o[:], ps[pi][:, :], [bps[pi]], [bo])
                        store_fm(r, cv_d, i * 128, 128, tt, o, bo, bd['cv'])
                return ([(O_CGATE + i * 128, 128), (O_CVAL + i * 128, 128)], evac, post)

            def mk_plain(c0, dst, bdst, row0, n=128, func=None):
                def evac(tt, pis):
                    o, bo = get_ot(r)
                    if func is None:
                        evac_copy(o[0:n, :], ps[pis[0]][0:n, :], [bps[pis[0]]], [bo])
                    else:
                        op('act', lambda e: e.activation(out=o[0:n, :], in_=ps[pis[0]][0:n, :], func=func), reads=[bps[pis[0]]], writes=[bo])
                    store_fm(r, dst, row0, n, tt, o, bo, bdst)
                return ([(c0, n)], evac, None)

            def mk_gg(i):
                def evac(tt, pis):
                    t1, bt1 = get_tmp(r)
                    t2, bt2 = get_tmp(r)
                    o, bo = get_ot(r)
                    op('act', lambda e: e.activation(out=t1[:], in_=ps[pis[0]][:, :], func=AF.Silu), reads=[bps[pis[0]]], writes=[bt1])
                    op('act', lambda e: e.activation(out=t2[:], in_=ps[pis[1]][:, :], func=AF.Sigmoid), reads=[bps[pis[1]]], writes=[bt2])
                    op('pool', lambda e: e.tensor_tensor(out=o[:], in0=t1[:], in1=t2[:], op=ALU.mult), reads=[bt1, bt2], writes=[bo])
                    store_fm(r, gg_d, i * 128, 128, tt, o, bo, bd['gg'])
                return ([(O_GOG + i * 128, 128), (O_MG + i * 128, 128)], evac, None)

            for i in range(8):
                jobs.append(mk_conv(i))
            for i in range(4):
                jobs.append(mk_plain(O_GQ + i * 128, gq_d, bd['gq'], i * 128))
            for i in range(4):
                jobs.append(mk_plain(O_GK + i * 128, gk_d, bd['gk'], i * 128))
            jobs.append(mk_plain(O_GF, gf_d, bd['gf'], 0, n=16))
            for i in range(8):
                jobs.append(mk_gg(i))
            for i in range(8):
                jobs.append(mk_plain(O_DQ + i * 128, dq_d, bd['dq'], i * 128))
            for i in range(8):
                jobs.append(mk_plain(O_DK + i * 128, dk_d, bd['dk'], i * 128))
            for i in range(8):
                jobs.append(mk_plain(O_MC + i * 128, mc_d, bd['mc'], i * 128, func=AF.Sigmoid))
            for i in range(8):
                jobs.append(mk_plain(O_MD + i * 128, md_d, bd['md'], i * 128, func=AF.Sigmoid))
            fm_run(r, Wl, jobs)

            tmjobs = [(O_GV, gv_d, bd['gv'], 0), (O_GV + 512, gv_d, bd['gv'], 512), (O_DV, dv_d, bd['dv'], 0),
                      (O_DV + 512, dv_d, bd['dv'], 512), (O_GK, gkt_d, bd['gkt'], 0)]
            for si, (c0, dst, bdst, dc0) in enumerate(tmjobs):
                op('sp', lambda e: e.dma_start(out=tms[:], in_=Wl[:, c0:c0 + 512].rearrange("(kc p) f -> p kc f", p=128)), writes=[btms], chan='wld')
                wb = tmb[si % 2]
                op('pool', lambda e: e.tensor_copy(out=wb[:], in_=tms[:]), reads=[btms], writes=[btmb[si % 2]])
                for tb in range(NB):
                    pi = next_ps()
                    for kc in range(KC):
                        op('pe', lambda e: e.matmul(ps[pi][:, :], lhsT=hT[:, kc, tb * 128:(tb + 1) * 128], rhs=wb[:, kc, :],
                                                    start=(kc == 0), stop=(kc == KC - 1)),
                           reads=[btmb[si % 2], bhT[tb // 4]], writes=[bps[pi]], sig=(kc == KC - 1))
                    oi = (si * NB + tb) % 4
                    evac_copy(tmo[oi][:], ps[pi][:, :], [bps[pi]], [btmo[oi]])
                    op('sp', lambda e: e.dma_start(out=dst[tb * 128:(tb + 1) * 128, dc0:dc0 + 512], in_=tmo[oi][:]), reads=[btmo[oi]], writes=[bdst], chan='st')
            fw.barrier()


    def phaseG(l):
        with ExitStack() as st:
            mask01 = alloc(st, "g_mask01", [128, 4, 128], F32)
            triT = alloc(st, "g_triT", [128, 4, 128], F32)
            U = alloc(st, "g_U", [128, 128], F32)
            bfb = alloc(st, "g_bfb", [128, 512], F32)
            wfu = alloc(st, "g_wfu", [16, 512], F32)
            wfu_bf = alloc(st, "g_wfub", [16, 512], BF16)
            fT = alloc(st, "g_fT", [16, S], BF16)
            bc = Buf()
            qgb = [alloc(st, "g_qg%d" % i, [128, 4, 512], BF16) for i in range(2)]
            kgb = [alloc(st, "g_kg%d" % i, [128, 4, 512], BF16) for i in range(2)]
            kdb = [alloc(st, "g_kd%d" % i, [128, 4, 512], BF16) for i in range(2)]
            decb = [alloc(st, "g_dec%d" % i, [128, 4, 4], F32) for i in range(2)]
            bqk = [[Buf() for _ in range(2)] for _ in range(4)]
            bkd = [Buf() for _ in range(2)]
            S_f = [alloc(st, "g_Sf%d" % h, [128, 256], F32) for h in range(4)]
            S_b = [alloc(st, "g_Sb%d" % h, [128, 256], BF16) for h in range(4)]
            bS = [Buf() for _ in range(4)]
            raw = [alloc(st, "g_raw%d" % i, [128, 512], BF16) for i in range(4)]
            braw = [Buf() for _ in range(4)]
            f32t = [alloc(st, "g_f32t%d" % i, [128, 512], F32) for i in range(6)]
            bf32t = [Buf() for _ in range(6)]
            ktm = [alloc(st, "g_ktm%d" % i, [128, 4, 512], BF16) for i in range(2)]
            bktm = [Buf() for _ in range(2)]
            vt = [alloc(st, "g_vt%d" % i, [128, 4, 1024], BF16) for i in range(2)]
            bvt = [Buf() for _ in range(2)]
            att = [alloc(st, "g_att%d" % i, [128, 4, 128], BF16) for i in range(2)]
            batt = [Buf() for _ in range(2)]
            sqt = [alloc(st, "g_sq%d" % i, [128, 512], BF16) for i in range(2)]
            bsq = [Buf() for _ in range(2)]
            ggt = [alloc(st, "g_ggt%d" % i, [128, 512], BF16) for i in range(4)]
            bggt = [Buf() for _ in range(4)]
            yo = [alloc(st, "g_yo%d" % i, [128, 512], BF16) for i in range(4)]
            byo = [Buf() for _ in range(4)]
            cnt = {'f': 0, 'r': 0, 'g': 0, 'y': 0}

            def f32():
                i = cnt['f'] % 6
                cnt['f'] += 1
                return f32t[i], bf32t[i]

            def rawt():
                i = cnt['r'] % 4
                cnt['r'] += 1
                return raw[i], braw[i]

            op('pool', lambda e: e.memset(mask01[:], 1.0), writes=[bc])
            op('pool', lambda e: e.memset(mask01[:, :, 0:1], 0.0), writes=[bc])
            op('pool', lambda e: e.memset(triT[:], 1.0), writes=[bc])
            op('pool', lambda e: e.affine_select(out=triT[:], in_=triT[:], pattern=[[0, 4], [1, 128]], base=0, channel_multiplier=-1,
                                                 compare_op=ALU.is_ge, fill=0.0), reads=[bc], writes=[bc])
            op('pool', lambda e: e.memset(U[:], 1.0), writes=[bc])
            op('pool', lambda e: e.affine_select(out=U[:], in_=U[:], pattern=[[-1, 128]], base=0, channel_multiplier=1,
                                                 compare_op=ALU.is_gt, fill=0.0), reads=[bc], writes=[bc])
            with nc.allow_non_contiguous_dma(reason="bias broadcast"):
                op('sp', lambda e: e.dma_start(out=bfb[:], in_=gla_bf[l:l + 1, :].broadcast_to([128, 512])), writes=[bc], chan='cst')
            op('sp', lambda e: e.dma_start(out=wfu[:], in_=gla_wf_up[l]), writes=[bc], chan='cst')
            op('dve', lambda e: e.tensor_copy(out=wfu_bf[:], in_=wfu[:]), reads=[bc], writes=[bc])
            op('sp', lambda e: e.dma_start(out=fT[:], in_=gf_d[:, :]), reads=[bd['gf']], writes=[bc], chan='ld')
            for h in range(4):
                op('pool', lambda e: e.memset(S_f[h][:], 0.0), writes=[bS[h]])
                op('pool', lambda e: e.memset(S_b[h][:], 0.0), writes=[bS[h]])

            def prep(tt):
                tsl = slice(tt * 512, (tt + 1) * 512)
                qg, kg, kd, dec = qgb[tt % 2], kgb[tt % 2], kdb[tt % 2], decb[tt % 2]
                for h in range(4):
                    pi = next_ps()
                    op('pe', lambda e: e.matmul(ps[pi][:, :], lhsT=wfu_bf[0:16, h * 128:(h + 1) * 128], rhs=fT[0:16, tsl], start=True, stop=True),
                       reads=[bc], writes=[bps[pi]])
                    E, bE = f32()
                    op('act', lambda e: e.activation(out=E[:], in_=ps[pi][:, :], func=AF.Exp, bias=nbf[:, l, h:h + 1], scale=-1.0),
                       reads=[bps[pi], b_const], writes=[bE])
                    op('act', lambda e: e.activation(out=E[:], in_=E[:], func=AF.Ln, bias=one_t[:], scale=1.0), reads=[bE, b_const], writes=[bE])
                    Gp, bG = f32()
                    op('dve', lambda e: e.tensor_tensor_scan(out=Gp[:], data0=mask01[:].rearrange("p a b -> p (a b)"), data1=E[:], initial=0.0,
                                                             op0=ALU.mult, op1=ALU.add), reads=[bE, bc], writes=[bG])
                    EG, bEG = f32()
                    op('act', lambda e: e.activation(out=EG[:], in_=Gp[:], func=AF.Exp, bias=lnsc_t[:], scale=-1.0 / 16), reads=[bG, b_const], writes=[bEG])
                    qr, bqr = rawt()
                    op('sp', lambda e: e.dma_start(out=qr[:], in_=gq_d[h * 128:(h + 1) * 128, tsl]), reads=[bd['gq']], writes=[bqr], chan='ld')
                    op('dve', lambda e: e.tensor_tensor(out=qg[:, h, :], in0=qr[:], in1=EG[:], op=ALU.mult), reads=[bqr, bEG], writes=[bqk[h][tt % 2]])
                    EN, bEN = f32()
                    op('act', lambda e: e.activation(out=EN[:], in_=Gp[:], func=AF.Exp, scale=1.0 / 16), reads=[bG], writes=[bEN])
                    kr, bkr = rawt()
                    op('sp', lambda e: e.dma_start(out=kr[:], in_=gk_d[h * 128:(h + 1) * 128, tsl]), reads=[bd['gk']], writes=[bkr], chan='ld')
                    op('dve', lambda e: e.tensor_tensor(out=kg[:, h, :], in0=kr[:], in1=EN[:], op=ALU.mult), reads=[bkr, bEN], writes=[bqk[h][tt % 2]])
                    op('act', lambda e: e.activation(out=dec[:, h, :], in_=Gp[:].rearrange("p (b t) -> p b t", t=128)[:, :, 127],
                                                     func=AF.Exp, scale=-1.0 / 16), reads=[bG], writes=[bqk[h][tt % 2]])
                kt = ktm[tt % 2]
                op('sp', lambda e: e.dma_start(out=kt[:], in_=gkt_d[tsl, :].rearrange("(b p) f -> p b f", p=128)), reads=[bd['gkt']], writes=[bktm[tt % 2]], chan='ld')
                for blk in range(4):
                    tb = tt * 4 + blk
                    pi = next_ps()
                    op('pe', lambda e: e.matmul(ps[pi][:, :], lhsT=fT[0:16, tb * 128:(tb + 1) * 128], rhs=wfu_bf[0:16, :], start=True, stop=True),
                       reads=[bc], writes=[bps[pi]])
                    Z, bZ = f32()
                    op('dve', lambda e: e.tensor_tensor(out=Z[:], in0=ps[pi][:, :], in1=bfb[:], op=ALU.add), reads=[bps[pi], bc], writes=[bZ])
                    op('act', lambda e: e.activation(out=Z[:], in_=Z[:], func=AF.Exp, scale=-1.0), reads=[bZ], writes=[bZ])
                    op('act', lambda e: e.activation(out=Z[:], in_=Z[:], func=AF.Ln, bias=one_t[:], scale=1.0), reads=[bZ, b_const], writes=[bZ])
                    pi2 = next_ps()
                    op('pe', lambda e: e.matmul(ps[pi2][:, :], lhsT=U[:], rhs=Z[:], start=True, stop=True), reads=[bc, bZ], writes=[bps[pi2]])
                    ED, bED = f32()
                    op('act', lambda e: e.activation(out=ED[:], in_=ps[pi2][:, :], func=AF.Exp, scale=-1.0 / 16), reads=[bps[pi2]], writes=[bED])
                    op('dve', lambda e: e.tensor_tensor(out=kd[:, blk, :], in0=kt[:, blk, :], in1=ED[:], op=ALU.mult), reads=[bktm[tt % 2], bED], writes=[bkd[tt % 2]])

            def recur(tt):
                tsl = slice(tt * 512, (tt + 1) * 512)
                qg, kg, kd, dec = qgb[tt % 2], kgb[tt % 2], kdb[tt % 2], decb[tt % 2]
                v = vt[tt % 2]
                op('sp', lambda e: e.dma_start(out=v[:], in_=gv_d[tsl, :].rearrange("(b p) f -> p b f", p=128)), reads=[bd['gv']], writes=[bvt[tt % 2]], chan='ld')
                for pair in range(2):
                    heads = (2 * pair, 2 * pair + 1)
                    for hi, h in enumerate(heads):
                        a = att[hi]
                        for blk in range(4):
                            bsl = slice(blk * 128, (blk + 1) * 128)
                            op('pe', lambda e: e.matmul(ps[4][:, blk * 128:(blk + 1) * 128], lhsT=kg[:, h, bsl], rhs=qg[:, h, bsl], start=True, stop=True),
                               reads=[bqk[h][tt % 2]], writes=[bps[4]], sig=(blk == 3))
                        op('dve', lambda e: e.tensor_tensor(out=a[:], in0=ps[4][:, :].rearrange("p (a b) -> p a b", a=4), in1=triT[:], op=ALU.mult),
                           reads=[bps[4], bc], writes=[batt[hi]])
                    for blk in range(4):
                        tb = tt * 4 + blk
                        bsl = slice(blk * 128, (blk + 1) * 128)
                        for hi, h in enumerate(heads):
                            for vh in range(2):
                                pb = hi * 2 + vh
                                op('pe', lambda e: e.matmul(ps[pb][:, blk * 128:(blk + 1) * 128], lhsT=v[:, blk, h * 256 + vh * 128:h * 256 + (vh + 1) * 128],
                                                            rhs=att[hi][:, blk, :], start=True, stop=False),
                                   reads=[bvt[tt % 2], batt[hi]], writes=[bps[pb]], sig=False)
                                op('pe', lambda e: e.matmul(ps[pb][:, blk * 128:(blk + 1) * 128], lhsT=S_b[h][:, vh * 128:(vh + 1) * 128],
                                                            rhs=qg[:, h, bsl], start=False, stop=True),
                                   reads=[bS[h], bqk[h][tt % 2]], writes=[bps[pb]], sig=True)
                            pd = 5 + hi
                            op('pe', lambda e: e.matmul(ps[pd][:, 0:256], lhsT=kd[:, blk, h * 128:(h + 1) * 128], rhs=v[:, blk, h * 256:(h + 1) * 256], start=True, stop=True),
                               reads=[bkd[tt % 2], bvt[tt % 2]], writes=[bps[pd]])
                            op('dve', lambda e: e.scalar_tensor_tensor(out=S_f[h][:], in0=S_f[h][:], scalar=dec[:, h, blk:blk + 1], in1=ps[pd][:, 0:256],
                                                                       op0=ALU.mult, op1=ALU.add),
                               reads=[bS[h], bps[pd], bqk[h][tt % 2]], writes=[bS[h]])
                            op('dve', lambda e: e.tensor_copy(out=S_b[h][:], in_=S_f[h][:]), reads=[bS[h]], writes=[bS[h]])
                    for hi, h in enumerate(heads):
                        for vh in range(2):
                            pb = hi * 2 + vh
                            op('act', lambda e: e.activation(out=sqt[vh][:], in_=ps[pb][:, :], func=AF.Square), reads=[bps[pb]], writes=[bsq[vh]])
                            op('pe', lambda e: e.matmul(ps[7][:, :], lhsT=ones_bf[:], rhs=sqt[vh][:], start=(vh == 0), stop=(vh == 1)),
                               reads=[bsq[vh], b_const], writes=[bps[7]], sig=(vh == 1))
                        sd, bsd = f32()
                        op('act', lambda e: e.activation(out=sd[:], in_=ps[7][:, :], func=AF.Sqrt, bias=eps_t[:], scale=1.0 / 256), reads=[bps[7], b_const], writes=[bsd])
                        op('dve', lambda e: e.reciprocal(out=sd[:], in_=sd[:]), reads=[bsd], writes=[bsd])
                        for vh in range(2):
                            pb = hi * 2 + vh
                            gi = cnt['g'] % 4
                            cnt['g'] += 1
                            r0 = h * 256 + vh * 128
                            op('sp', lambda e: e.dma_start(out=ggt[gi][:], in_=gg_d[r0:r0 + 128, tsl]), reads=[bd['gg']], writes=[bggt[gi]], chan='ld')
                            y, by = f32()
                            op('dve', lambda e: e.scalar_tensor_tensor(out=y[:], in0=ps[pb][:, :], scalar=gng[:, l, vh:vh + 1], in1=sd[:], op0=ALU.mult, op1=ALU.mult),
                               reads=[bps[pb], bsd, b_const], writes=[by])
                            yi = cnt['y'] % 4
                            cnt['y'] += 1
                            op('pool', lambda e: e.tensor_tensor(out=yo[yi][:], in0=y[:], in1=ggt[gi][:], op=ALU.mult), reads=[by, bggt[gi]], writes=[byo[yi]])
                            op('sp', lambda e: e.dma_start(out=yg_d[r0:r0 + 128, tsl], in_=yo[yi][:]), reads=[byo[yi]], writes=[bd['yg']], chan='st')

            prep(0)
            for tt in range(NT):
                if tt + 1 < NT:
                    prep(tt + 1)
                recur(tt)
            fw.barrier()


    def phaseD(l):
        with ExitStack() as st:
            QT = [alloc(st, "d_QT%d" % i, [128, S], BF16) for i in range(2)]
            KT = [alloc(st, "d_KT%d" % i, [128, S], BF16) for i in range(2)]
            V = [alloc(st, "d_V%d" % i, [128, NB, 128], BF16) for i in range(2)]
            bqkv = [Buf() for _ in range(2)]
            pt = [[alloc(st, "d_pt%d_%d" % (m, i), [128, 512], BF16) for i in range(3)] for m in range(2)]
            bpt = [[Buf() for _ in range(3)] for _ in range(2)]
            f32t = [alloc(st, "d_f32t%d" % i, [128, 512], F32) for i in range(6)]
            bf32t = [Buf() for _ in range(6)]
            sqt = alloc(st, "d_sq", [128, 512], BF16)
            bsq = Buf()
            mdt = [alloc(st, "d_md%d" % i, [128, 512], BF16) for i in range(2)]
            bmdt = [Buf() for _ in range(2)]
            yo = [alloc(st, "d_yo%d" % i, [128, 512], BF16) for i in range(2)]
            byo = [Buf() for _ in range(2)]
            cnt = {'f': 0, 'p': 0, 's': 0, 'y': 0}

            def f32():
                i = cnt['f'] % 6
                cnt['f'] += 1
                return f32t[i], bf32t[i]

            for h in range(8):
                hb = h % 2
                Q, K_, Vh = QT[hb], KT[hb], V[hb]
                op('sp', lambda e: e.dma_start(out=Q[:], in_=dq_d[h * 128:(h + 1) * 128, :]), reads=[bd['dq']], writes=[bqkv[hb]], chan='ld')
                op('sp', lambda e: e.dma_start(out=K_[:], in_=dk_d[h * 128:(h + 1) * 128, :]), reads=[bd['dk']], writes=[bqkv[hb]], chan='ld')
                op('sp', lambda e: e.dma_start(out=Vh[:], in_=dv_d[:, h * 128:(h + 1) * 128].rearrange("(b p) f -> p b f", p=128)),
                   reads=[bd['dv']], writes=[bqkv[hb]], chan='ld')
                for qt in range(NT):
                    tsl0 = qt * 512
                    nkb = 4 * (qt + 1)
                    mi = (h * NT + qt) % 2
                    op('sp', lambda e: e.dma_start(out=mdt[mi][:], in_=md_d[h * 128:(h + 1) * 128, tsl0:tsl0 + 512]), reads=[bd['md']], writes=[bmdt[mi]], chan='ld')

                    def c0_of(kb):
                        return 128 * (kb - 4 * qt) if kb >= 4 * qt else 0
                    sbank = {}
                    ptile = {}

                    def do_S(kb):
                        c0 = c0_of(kb)
                        si = cnt['s'] % 2
                        cnt['s'] += 1
                        sbank[kb] = (si, 2 + si)
                        for m in range(2):
                            pb = sbank[kb][m]
                            op('pe', lambda e: e.matmul(ps[pb][:, c0:512], lhsT=K_[m * 64:(m + 1) * 64, kb * 128:(kb + 1) * 128],
                                                        rhs=Q[m * 64:(m + 1) * 64, tsl0 + c0:tsl0 + 512], start=True, stop=True),
                               reads=[bqkv[hb]], writes=[bps[pb]])
                        pi = cnt['p'] % 3
                        cnt['p'] += 1
                        ptile[kb] = pi
                        for m in range(2):
                            pb = sbank[kb][m]
                            op('act', lambda e: e.activation(out=pt[m][pi][:, c0:512], in_=ps[pb][:, c0:512], func=AF.Exp, scale=0.125),
                               reads=[bps[pb]], writes=[bpt[m][pi]])
                            if kb >= 4 * qt:
                                op('pool', lambda e: e.memset(pt[m][pi][64:128, c0:c0 + 64], 0.0), reads=[], writes=[bpt[m][pi]])

                    def do_PV(kb):
                        c0 = c0_of(kb)
                        pi = ptile[kb]
                        first = (kb == 0)
                        last = (kb == nkb - 1)
                        for m in range(2):
                            op('pe', lambda e: e.matmul(ps[4 + m][:, c0:512], lhsT=Vh[:, kb, :], rhs=pt[m][pi][:, c0:512], start=first, stop=last),
                               reads=[bqkv[hb], bpt[m][pi]], writes=[bps[4 + m]], sig=last)
                            op('pe', lambda e: e.matmul(ps[6 + m][:, c0:512], lhsT=ones_bf[:], rhs=pt[m][pi][:, c0:512], start=first, stop=last),
                               reads=[b_const, bpt[m][pi]], writes=[bps[6 + m]], sig=True)

                    do_S(0)
                    for kb in range(nkb):
                        if kb + 1 < nkb:
                            do_S(kb + 1)
                        do_PV(kb)
                    r0, br0 = f32()
                    r1, br1 = f32()
                    op('dve', lambda e: e.reciprocal(out=r0[:], in_=ps[6][:, :]), reads=[bps[6]], writes=[br0])
                    op('dve', lambda e: e.reciprocal(out=r1[:], in_=ps[7][:, :]), reads=[bps[7]], writes=[br1])
                    op('dve', lambda e: e.tensor_tensor(out=r0[:], in0=ps[4][:, :], in1=r0[:], op=ALU.mult), reads=[bps[4], br0], writes=[br0])
                    op('dve', lambda e: e.tensor_tensor(out=r1[:], in0=ps[5][:, :], in1=r1[:], op=ALU.mult), reads=[bps[5], br1], writes=[br1])
                    O, bO = f32()
                    op('dve', lambda e: e.scalar_tensor_tensor(out=O[:], in0=r1[:], scalar=nlam[:, l:l + 1], in1=r0[:], op0=ALU.mult, op1=ALU.add),
                       reads=[br0, br1, b_const], writes=[bO])
                    op('act', lambda e: e.activation(out=sqt[:], in_=O[:], func=AF.Square), reads=[bO], writes=[bsq])
                    pn = cnt['s'] % 2
                    cnt['s'] += 1
                    op('pe', lambda e: e.matmul(ps[pn][:, :], lhsT=ones_bf[:], rhs=sqt[:], start=True, stop=True), reads=[b_const, bsq], writes=[bps[pn]])
                    sd, bsd = f32()
                    op('act', lambda e: e.activation(out=sd[:], in_=ps[pn][:, :], func=AF.Sqrt, bias=eps_t[:], scale=1.0 / 128), reads=[bps[pn], b_const], writes=[bsd])
                    op('dve', lambda e: e.reciprocal(out=sd[:], in_=sd[:]), reads=[bsd], writes=[bsd])
                    op('dve', lambda e: e.scalar_tensor_tensor(out=O[:], in0=O[:], scalar=dng[:, l:l + 1], in1=sd[:], op0=ALU.mult, op1=ALU.mult),
                       reads=[bO, bsd, b_const], writes=[bO])
                    yi = cnt['y'] % 2
                    cnt['y'] += 1
                    op('pool', lambda e: e.tensor_tensor(out=yo[yi][:], in0=O[:], in1=mdt[mi][:], op=ALU.mult), reads=[bO, bmdt[mi]], writes=[byo[yi]])
                    op('sp', lambda e: e.dma_start(out=yd_d[h * 128:(h + 1) * 128, tsl0:tsl0 + 512], in_=yo[yi][:]), reads=[byo[yi]], writes=[bd['yd']], chan='st')
            fw.barrier()

    def load_resident(st, pfx, Wl, K, ncols, dst, bdst, stage, bstage):
        kcn = K // 128
        for j in range(ncols // 128):
            sidx = j % len(stage)
            op('sp', lambda e: e.dma_start(out=stage[sidx][:, 0:kcn, :], in_=Wl[:, j * 128:(j + 1) * 128].rearrange("(kc p) f -> p kc f", p=128)),
               writes=[bstage[sidx]], chan='wld')
            op('pool', lambda e: e.tensor_copy(out=dst[:, :, j * 128:(j + 1) * 128], in_=stage[sidx][:, 0:kcn, :]), reads=[bstage[sidx]], writes=[bdst])

    def phaseM(l):
        with ExitStack() as st:
            stage = [alloc(st, "m_stg%d" % i, [128, KC, 128], F32) for i in range(2)]
            bstage = [Buf() for _ in range(2)]
            wpw = alloc(st, "m_wpw", [128, KC, D], BF16)
            wo = alloc(st, "m_wo", [128, KC, D], BF16)
            bwpw, bwo = Buf(), Buf()
            load_resident(st, "m", w_conv_out[l], D, D, wpw, bwpw, stage, bstage)
            load_resident(st, "m", w_out[l], D, D, wo, bwo, stage, bstage)
            cvt = [alloc(st, "m_cvt%d" % i, [128, KC, 512], BF16) for i in range(1)] * 2
            bcvt = [Buf()] * 2
            mct = alloc(st, "m_mct", [128, KC, 512], BF16)
            ygt = alloc(st, "m_ygt", [128, KC, 512], BF16)
            ydt = alloc(st, "m_ydt", [128, KC, 512], BF16)
            bmct, bygt, bydt = Buf(), Buf(), Buf()
            xt = alloc(st, "m_xt", [128, KC, 512], F32)
            bxt = Buf()
            sqc = alloc(st, "m_sqc", [128, KC, 512], BF16)
            bsqc = Buf()
            actt = alloc(st, "m_act", [128, KC, 512], BF16)
            bact = Buf()
            yt = alloc(st, "m_yt", [128, KC, 512], BF16)
            byt = Buf()
            f32t = [alloc(st, "m_f32t%d" % i, [128, 512], F32) for i in range(8)]
            bf32t = [Buf() for _ in range(8)]
            sd = alloc(st, "m_sd", [128, 512], F32)
            bsd = Buf()
            cnt = {'f': 0}

            def f32():
                i = 4 + cnt['f'] % 4
                cnt['f'] += 1
                return f32t[i], bf32t[i]

            def fm_tile(dst, tt):
                return dst[:, tt * 512:(tt + 1) * 512].rearrange("(c p) t -> p c t", p=128)

            for tt in range(NT):
                cv = cvt[tt % 2]
                op('sp', lambda e: e.dma_start(out=cv[:], in_=fm_tile(cv_d, tt)), reads=[bd['cv']], writes=[bcvt[tt % 2]], chan='ld')
                op('sp', lambda e: e.dma_start(out=mct[:], in_=fm_tile(mc_d, tt)), reads=[bd['mc']], writes=[bmct], chan='ld')
                op('sp', lambda e: e.dma_start(out=ygt[:], in_=fm_tile(yg_d, tt)), reads=[bd['yg']], writes=[bygt], chan='ld')
                op('sp', lambda e: e.dma_start(out=ydt[:], in_=fm_tile(yd_d, tt)), reads=[bd['yd']], writes=[bydt], chan='ld')
                op('sp', lambda e: e.dma_start(out=xt[:], in_=fm_tile(xT_d, tt)), reads=[bx_d[tt]], writes=[bxt], chan='ld')
                op('act', lambda e: e.activation(out=sqc[:], in_=cv[:], func=AF.Square), reads=[bcvt[tt % 2]], writes=[bsqc])
                p1 = next_ps()
                for c in range(KC):
                    op('pe', lambda e: e.matmul(ps[p1][:, :], lhsT=ones_bf[:], rhs=cv[:, c, :], start=(c == 0), stop=(c == KC - 1)),
                       reads=[bcvt[tt % 2], b_const], writes=[bps[p1]], sig=(c == KC - 1))
                p2 = next_ps()
                for c in range(KC):
                    op('pe', lambda e: e.matmul(ps[p2][:, :], lhsT=ones_bf[:], rhs=sqc[:, c, :], start=(c == 0), stop=(c == KC - 1)),
                       reads=[bsqc, b_const], writes=[bps[p2]], sig=(c == KC - 1))
                mean, bmean = f32t[0], bf32t[0]
                op('act', lambda e: e.activation(out=mean[:], in_=ps[p1][:, :], func=AF.Copy, scale=1.0 / D), reads=[bps[p1]], writes=[bmean])
                msq, bmsq = f32t[1], bf32t[1]
                op('dve', lambda e: e.tensor_tensor(out=msq[:], in0=mean[:], in1=mean[:], op=ALU.mult), reads=[bmean], writes=[bmsq])
                var, bvar = f32t[2], bf32t[2]
                op('dve', lambda e: e.scalar_tensor_tensor(out=var[:], in0=ps[p2][:, :], scalar=1.0 / D, in1=msq[:], op0=ALU.mult, op1=ALU.subtract),
                   reads=[bps[p2], bmsq], writes=[bvar])
                op('dve', lambda e: e.tensor_scalar(out=var[:], in0=var[:], scalar1=0.0, scalar2=None, op0=ALU.max), reads=[bvar], writes=[bvar])
                op('act', lambda e: e.activation(out=var[:], in_=var[:], func=AF.Sqrt, bias=eps_t[:], scale=1.0), reads=[bvar, b_const], writes=[bvar])
                op('dve', lambda e: e.reciprocal(out=var[:], in_=var[:]), reads=[bvar], writes=[bvar])
                nb_, bnb = f32t[3], bf32t[3]
                op('dve', lambda e: e.scalar_tensor_tensor(out=nb_[:], in0=mean[:], scalar=-1.0, in1=var[:], op0=ALU.mult, op1=ALU.mult),
                   reads=[bmean, bvar], writes=[bnb])
                for c in range(KC):
                    t1, bt1 = f32()
                    op('pool', lambda e: e.tensor_tensor(out=t1[:], in0=cv[:, c, :], in1=var[:], op=ALU.mult), reads=[bcvt[tt % 2], bvar], writes=[bt1])
                    op('dve', lambda e: e.tensor_tensor(out=t1[:], in0=t1[:], in1=nb_[:], op=ALU.add), reads=[bt1, bnb], writes=[bt1])
                    op('act', lambda e: e.activation(out=actt[:, c, :], in_=t1[:], func=AF.Silu, bias=cln_b[:, l, c:c + 1], scale=cln_g[:, l, c:c + 1]),
                       reads=[bt1, b_const], writes=[bact])
                op('pool', lambda e: e.tensor_tensor(out=ygt[:], in0=ygt[:], in1=ydt[:], op=ALU.add), reads=[bygt, bydt], writes=[bygt])
                for fc in range(KC):
                    pi = next_ps()
                    for kc in range(KC):
                        op('pe', lambda e: e.matmul(ps[pi][:, :], lhsT=wpw[:, kc, fc * 128:(fc + 1) * 128], rhs=actt[:, kc, :], start=(kc == 0), stop=(kc == KC - 1)),
                           reads=[bwpw, bact], writes=[bps[pi]], sig=(kc == KC - 1))
                    t1, bt1 = f32()
                    op('dve', lambda e: e.tensor_tensor(out=t1[:], in0=ps[pi][:, :], in1=mct[:, fc, :], op=ALU.mult), reads=[bps[pi], bmct], writes=[bt1])
                    op('dve', lambda e: e.tensor_tensor(out=yt[:, fc, :], in0=t1[:], in1=ygt[:, fc, :], op=ALU.add), reads=[bt1, bygt], writes=[byt])
                for fc in range(KC):
                    pi = next_ps()
                    for kc in range(KC):
                        op('pe', lambda e: e.matmul(ps[pi][:, :], lhsT=wo[:, kc, fc * 128:(fc + 1) * 128], rhs=yt[:, kc, :], start=(kc == 0), stop=(kc == KC - 1)),
                           reads=[bwo, byt], writes=[bps[pi]], sig=(kc == KC - 1))
                    op('dve', lambda e: e.tensor_tensor(out=xt[:, fc, :], in0=ps[pi][:, :], in1=xt[:, fc, :], op=ALU.add), reads=[bps[pi], bxt], writes=[bxt])
                if dbg:
                    op('sp', lambda e: e.dma_start(out=dbg_act[:, :, tt * 512:(tt + 1) * 512], in_=actt[:]), reads=[bact], writes=[Buf()], chan='st')
                    op('sp', lambda e: e.dma_start(out=dbg_y[:, :, tt * 512:(tt + 1) * 512], in_=yt[:]), reads=[byt], writes=[Buf()], chan='st')
                    op('sp', lambda e: e.dma_start(out=dbg_rstd[:, tt * 512:(tt + 1) * 512], in_=var[:]), reads=[bvar], writes=[Buf()], chan='st')
                    op('sp', lambda e: e.dma_start(out=dbg_mean[:, tt * 512:(tt + 1) * 512], in_=mean[:]), reads=[bmean], writes=[Buf()], chan='st')
                op('sp', lambda e: e.dma_start(out=fm_tile(xT_d, tt), in_=xt[:]), reads=[bxt], writes=[bx_d[tt]], chan='st')
                rmsnorm_tile((sqc, bsqc, sd, bsd), xt, bxt, g_ffn[:, l, :], tt)
            fw.barrier()

    def phaseF1(l):
        with ExitStack() as st:
            r = fm_resources(st, "f1")

            def mk(j):
                def evac(tt, pis):
                    t, bt = get_tmp(r)
                    o, bo = get_ot(r)
                    op('act', lambda e: e.activation(out=t[:], in_=ps[pis[0]][:, :], func=AF.Silu), reads=[bps[pis[0]]], writes=[bt])
                    op('dve', lambda e: e.tensor_tensor(out=o[:], in0=ps[pis[1]][:, :], in1=t[:], op=ALU.mult), reads=[bps[pis[1]], bt], writes=[bo])
                    store_fm(r, aT_d, j * 128, 128, tt, o, bo, bd['aT'])
                return ([(j * 128, 128), (FH + j * 128, 128)], evac, None)
            fm_run(r, w_ffn_in[l], [mk(j) for j in range(FH // 128)])
            fw.barrier()

    def phaseF2(l):
        KF = FH // 128
        last = (l == L - 1)
        with ExitStack() as st:
            stage = [alloc(st, "f2_stg%d" % i, [128, KF, 128], F32) for i in range(1)]
            bstage = [Buf() for _ in range(1)]
            wfo = alloc(st, "f2_wfo", [128, KF, D], BF16)
            bwfo = Buf()
            load_resident(st, "f2", w_ffn_out[l], FH, D, wfo, bwfo, stage, bstage)
            at = [alloc(st, "f2_at%d" % i, [128, KF, 512], BF16) for i in range(1)] * 2
            bat = [Buf()] * 2
            xt = [alloc(st, "f2_xt%d" % i, [128, KC, 512], F32) for i in range(1)] * 2
            bxt = [Buf()] * 2
            sq = alloc(st, "f2_sq", [128, KC, 512], BF16)
            sd = alloc(st, "f2_sd", [128, 512], F32)
            bsq, bsd = Buf(), Buf()
            if last:
                fin = alloc(st, "f2_fin", [128, KC, 512], F32)
                bfin = Buf()
                otm = [alloc(st, "f2_otm%d" % i, [128, D], F32) for i in range(2)]
                botm = [Buf() for _ in range(2)]
            b_out = Buf()
            for tt in range(NT):
                a = at[tt % 2]
                x_ = xt[tt % 2]
                op('sp', lambda e: e.dma_start(out=a[:], in_=aT_d[:, tt * 512:(tt + 1) * 512].rearrange("(c p) t -> p c t", p=128)),
                   reads=[bd['aT']], writes=[bat[tt % 2]], chan='ld')
                op('sp', lambda e: e.dma_start(out=x_[:], in_=xT_d[:, tt * 512:(tt + 1) * 512].rearrange("(c p) t -> p c t", p=128)),
                   reads=[bx_d[tt]], writes=[bxt[tt % 2]], chan='ld')
                for fc in range(KC):
                    pi = next_ps()
                    for kc in range(KF):
                        op('pe', lambda e: e.matmul(ps[pi][:, :], lhsT=wfo[:, kc, fc * 128:(fc + 1) * 128], rhs=a[:, kc, :], start=(kc == 0), stop=(kc == KF - 1)),
                           reads=[bwfo, bat[tt % 2]], writes=[bps[pi]], sig=(kc == KF - 1))
                    op('dve', lambda e: e.tensor_tensor(out=x_[:, fc, :], in0=ps[pi][:, :], in1=x_[:, fc, :], op=ALU.add), reads=[bps[pi], bxt[tt % 2]], writes=[bxt[tt % 2]])
                if not last:
                    op('sp', lambda e: e.dma_start(out=xT_d[:, tt * 512:(tt + 1) * 512].rearrange("(c p) t -> p c t", p=128), in_=x_[:]),
                       reads=[bxt[tt % 2]], writes=[bx_d[tt]], chan='st')
                    rmsnorm_tile((sq, bsq, sd, bsd), x_, bxt[tt % 2], g_mix[:, l + 1, :], tt)
                else:
                    if dbg:
                        op('sp', lambda e: e.dma_start(out=xT_d[:, tt * 512:(tt + 1) * 512].rearrange("(c p) t -> p c t", p=128), in_=x_[:]),
                           reads=[bxt[tt % 2]], writes=[bx_d[tt]], chan='st')
                    rmsnorm_tile((sq, bsq, sd, bsd), x_, bxt[tt % 2], g_fin, tt, fin_out=fin, b_fin=bfin)
                    for j in range(4):
                        tb = tt * 4 + j
                        o_ = otm[tb % 2]
                        for half in range(2):
                            pi = next_ps()
                            for q in range(4):
                                c = half * 4 + q
                                op('pe', lambda e: e.transpose(ps[pi][:, q * 128:(q + 1) * 128], fin[:, c, j * 128:(j + 1) * 128], ident[:]),
                                   reads=[bfin, b_const], writes=[bps[pi]], sig=(q == 3))
                            evac_copy(o_[:, half * 512:(half + 1) * 512], ps[pi][:, :], [bps[pi]], [botm[tb % 2]])
                        op('sp', lambda e: e.dma_start(out=out[tb * 128:(tb + 1) * 128, :], in_=o_[:]), reads=[botm[tb % 2]], writes=[b_out], chan='out')
            fw.barrier()

    def finish():
        fw.barrier()
        fw.close()
        top.close()
        return nc

    for l in range(L):
        phaseP(l)
        if stop_after == 'P':
            return finish()
        phaseG(l)
        if stop_after == 'G':
            return finish()
        phaseD(l)
        if stop_after == 'D':
            return finish()
        phaseM(l)
        if stop_after == 'M':
            if dbg:
                dump_hT()
            return finish()
        phaseF1(l)
        phaseF2(l)
        if stop_after == 'F' and l == 0:
            return finish()
    return finish()


_NC_CACHE = {}


def kernel(**inputs):
    S, L, B = 4096, 4, 8
    if 'nc' not in _NC_CACHE:
        _NC_CACHE['nc'] = build_program(S=S, L=L)
    nc = _NC_CACHE['nc']
    x = np.ascontiguousarray(np.asarray(inputs['x'], dtype=np.float32))
    shared = {k: np.ascontiguousarray(np.asarray(v, dtype=np.float32)) for k, v in inputs.items() if k != 'x'}
    in_maps = []
    for b in range(B):
        m = dict(shared)
        m['x'] = x[b]
        in_maps.append(m)
    res = run_bass_kernel_spmd(nc, in_maps, core_ids=list(range(B)))
    return np.stack([np.asarray(res.results[b]['out'], dtype=np.float32) for b in range(B)], axis=0)
```

```python
import numpy as np
import concourse.bass as bass
import concourse.mybir as mybir
from concourse.bass_utils import run_bass_kernel_spmd
from contextlib import ExitStack

F32 = mybir.dt.float32
BF16 = mybir.dt.bfloat16
AF = mybir.ActivationFunctionType
ALU = mybir.AluOpType

SAME_ENGINE_SYNC = True


class Buf:
    __slots__ = ('w', 'r')

    def __init__(self):
        self.w = None
        self.r = {}


class FW:
    def __init__(self, nc):
        self.nc = nc
        self.eng = {'pe': nc.tensor, 'dve': nc.vector, 'act': nc.scalar, 'pool': nc.gpsimd, 'sp': nc.sync}
        self.stack = ExitStack()
        self.sem = {}
        self.cnt = {}
        self.waited = {e: {} for e in self.eng}
        for k in self.eng:
            self._mksem(k)
        self.nops = 0
        self.chan_n = {}
        self.CHAN_SLOTS = {'ld': 24, 'st': 24, 'wld': 8, 'cst': 4, 'out': 8}

    def _mksem(self, k):
        self.sem[k] = self.stack.enter_context(self.nc.semaphore("s_" + k))
        self.cnt[k] = 0

    def op(self, e, fn, reads=(), writes=(), sig=True, chan=None):
        waits = {}
        if chan:
            i = self.chan_n.get(chan, 0)
            self.chan_n[chan] = i + 1
            key = "%s_%d" % (chan, i % self.CHAN_SLOTS.get(chan, 8))
            if key not in self.sem:
                self._mksem(key)
            if self.cnt[key] > 0:
                waits[key] = self.cnt[key]
        else:
            key = e
        inc = 16 if chan else 1

        def need(tok):
            if tok is None:
                return
            k, v = tok
            if k == e and not chan and (e == 'pe' or not SAME_ENGINE_SYNC):
                return
            if waits.get(k, 0) < v:
                waits[k] = v

        for b in reads:
            need(b.w)
        for b in writes:
            need(b.w)
            for t in b.r.items():
                need(t)
        eng = self.eng[e]
        wd = self.waited[e]
        for k, v in waits.items():
            if wd.get(k, 0) >= v:
                continue
            eng.wait_ge(self.sem[k], v)
            wd[k] = v
        ins = fn(eng)
        self.nops += 1
        if sig:
            self.cnt[key] += inc
            ins.then_inc(self.sem[key], inc)
            tok = (key, self.cnt[key])
        else:
            tok = (key, self.cnt[key] + inc)
        for b in reads:
            if b.r.get(tok[0], 0) < tok[1]:
                b.r[tok[0]] = tok[1]
        for b in writes:
            b.w = tok
            b.r = {}
        return ins

    def barrier(self, engines=None):
        for e in (engines or self.eng):
            eng = self.eng[e]
            wd = self.waited[e]
            for k, v in self.cnt.items():
                if v > 0 and wd.get(k, 0) < v:
                    eng.wait_ge(self.sem[k], v)
                    wd[k] = v

    def close(self):
        self.stack.close()


D = 1024
KC = 8
INW = 11280
FH = 2816
EPS = 1e-6
O_CVAL, O_CGATE, O_GQ, O_GK, O_GV, O_GOG, O_GF = 0, 1024, 2048, 2560, 3072, 4096, 5120
O_DQ, O_DK, O_DV, O_MC, O_MG, O_MD = 5136, 6160, 7184, 8208, 9232, 10256
import math


def build_program(S=4096, L=4, dbg=False, stop_after=None):
    NT = S // 512
    NB = S // 128
    nc = bass.Bass("TRN2", target_bir_lowering=False)

    def din(name, shape):
        return nc.dram_tensor(name, shape, F32, kind="ExternalInput").ap()

    x_in = din("x", [S, D])
    norm_mix_g = din("norm_mix_g", [L, D])
    w_in = din("w_in", [L, D, INW])
    conv_dw = din("conv_dw", [L, 31, D])
    conv_ln_g = din("conv_ln_g", [L, D])
    conv_ln_b = din("conv_ln_b", [L, D])
    w_conv_out = din("w_conv_out", [L, D, D])
    gla_wf_up = din("gla_wf_up", [L, 16, 512])
    gla_bf = din("gla_bf", [L, 512])
    gla_norm_g = din("gla_norm_g", [L, 256])
    diff_lq1 = din("diff_lq1", [L, 64])
    diff_lk1 = din("diff_lk1", [L, 64])
    diff_lq2 = din("diff_lq2", [L, 64])
    diff_lk2 = din("diff_lk2", [L, 64])
    diff_norm_g = din("diff_norm_g", [L, 128])
    w_out = din("w_out", [L, D, D])
    norm_ffn_g = din("norm_ffn_g", [L, D])
    w_ffn_in = din("w_ffn_in", [L, D, 2 * FH])
    w_ffn_out = din("w_ffn_out", [L, FH, D])
    final_norm_g = din("final_norm_g", [D])
    out = nc.dram_tensor("out", [S, D], F32, kind="ExternalOutput").ap()

    skind = "ExternalOutput" if dbg else "Internal"

    def dscr(name, shape, dt):
        return nc.dram_tensor(name, shape, dt, kind=skind).ap()

    xT_d = dscr("s_xT", [D, S], F32)
    cv_d = dscr("s_cv", [D, S], BF16)
    gq_d = dscr("s_gq", [512, S], BF16)
    gk_d = dscr("s_gk", [512, S], BF16)
    gf_d = dscr("s_gf", [16, S], BF16)
    gg_d = dscr("s_gg", [D, S], BF16)
    dq_d = dscr("s_dq", [D, S], BF16)
    dk_d = dscr("s_dk", [D, S], BF16)
    mc_d = dscr("s_mc", [D, S], BF16)
    md_d = dscr("s_md", [D, S], BF16)
    gv_d = dscr("s_gv", [S, D], BF16)
    dv_d = dscr("s_dv", [S, D], BF16)
    gkt_d = dscr("s_gkt", [S, 512], BF16)
    yg_d = dscr("s_yg", [D, S], BF16)
    yd_d = dscr("s_yd", [D, S], BF16)
    aT_d = dscr("s_aT", [FH, S], BF16)

    fw = FW(nc)
    op = fw.op
    top = ExitStack()

    uid = [0]

    def alloc(st, name, shape, dt):
        uid[0] += 1
        return st.enter_context(nc.sbuf_tensor("%s_u%d" % (name, uid[0]), shape, dt))

    psbig = [top.enter_context(nc.psum_tensor("psb%d" % i, [128, 1024], F32)) for i in range(4)]
    ps = [psbig[i // 2][:, (i % 2) * 512:(i % 2 + 1) * 512] for i in range(8)]
    bps = [Buf() for _ in range(8)]
    hT = alloc(top, "hT", [128, KC, S], BF16)
    bhT = [Buf() for _ in range(NT)]
    ident = alloc(top, "ident", [128, 128], F32)
    ident_bf = alloc(top, "ident_bf", [128, 128], BF16)
    ones_bf = alloc(top, "ones_bf", [128, 128], BF16)
    b_const = Buf()
    g_mix = alloc(top, "g_mix", [128, L, KC], F32)
    g_ffn = alloc(top, "g_ffn", [128, L, KC], F32)
    g_fin = alloc(top, "g_fin", [128, KC], F32)
    cln_g = alloc(top, "cln_g", [128, L, KC], F32)
    cln_b = alloc(top, "cln_b", [128, L, KC], F32)
    nbf = alloc(top, "nbf", [128, L, 4], F32)
    gng = alloc(top, "gng", [128, L, 2], F32)
    dng = alloc(top, "dng", [128, L], F32)
    nlam = alloc(top, "nlam", [128, L], F32)
    lqk = alloc(top, "lqk", [128, 4, L, 64], F32)
    lsum = alloc(top, "lsum", [128, 2, L], F32)
    lprod = alloc(top, "lprod", [128, 2, L, 64], F32)
    eps_t = alloc(top, "eps_t", [128, 1], F32)
    one_t = alloc(top, "one_t", [128, 1], F32)
    lnsc_t = alloc(top, "lnsc_t", [128, 1], F32)

    pool_q = 'pool'

    with nc.allow_non_contiguous_dma(reason="tiny per-layer parameter vectors"):
        op('pool', lambda e: e.memset(ident[:], 1.0), writes=[b_const])
        op('pool', lambda e: e.affine_select(out=ident[:], in_=ident[:], pattern=[[-1, 128]], base=0,
                                             channel_multiplier=1, compare_op=ALU.is_equal, fill=0.0),
           reads=[b_const], writes=[b_const])
        op('pool', lambda e: e.tensor_copy(out=ident_bf[:], in_=ident[:]), reads=[b_const], writes=[b_const])
        op('pool', lambda e: e.memset(ones_bf[:], 1.0), writes=[b_const])
        op('pool', lambda e: e.memset(eps_t[:], EPS), writes=[b_const])
        op('pool', lambda e: e.memset(one_t[:], 1.0), writes=[b_const])
        op('pool', lambda e: e.memset(lnsc_t[:], math.log(128.0 ** -0.5)), writes=[b_const])
        for l in range(L):
            op('sp', lambda e: e.dma_start(out=g_mix[:, l, :], in_=norm_mix_g[l].rearrange("(c p) -> p c", p=128)), writes=[b_const], chan='cst')
            op('sp', lambda e: e.dma_start(out=g_ffn[:, l, :], in_=norm_ffn_g[l].rearrange("(c p) -> p c", p=128)), writes=[b_const], chan='cst')
            op('sp', lambda e: e.dma_start(out=cln_g[:, l, :], in_=conv_ln_g[l].rearrange("(c p) -> p c", p=128)), writes=[b_const], chan='cst')
            op('sp', lambda e: e.dma_start(out=cln_b[:, l, :], in_=conv_ln_b[l].rearrange("(c p) -> p c", p=128)), writes=[b_const], chan='cst')
            op('sp', lambda e: e.dma_start(out=nbf[:, l, :], in_=gla_bf[l].rearrange("(c p) -> p c", p=128)), writes=[b_const], chan='cst')
            op('sp', lambda e: e.dma_start(out=gng[:, l, :], in_=gla_norm_g[l].rearrange("(c p) -> p c", p=128)), writes=[b_const], chan='cst')
            op('sp', lambda e: e.dma_start(out=dng[:, l:l + 1], in_=diff_norm_g[l].rearrange("(c p) -> p c", p=128)), writes=[b_const], chan='cst')
            for i, t in enumerate([diff_lq1, diff_lk1, diff_lq2, diff_lk2]):
                op('sp', lambda e: e.dma_start(out=lqk[:, i, l, :], in_=t[l:l + 1, :].broadcast_to([128, 64])), writes=[b_const], chan='cst')
        op('sp', lambda e: e.dma_start(out=g_fin[:], in_=final_norm_g.rearrange("(c p) -> p c", p=128)), writes=[b_const], chan='cst')
        op('dve', lambda e: e.tensor_scalar(out=nbf[:], in0=nbf[:], scalar1=-1.0, scalar2=None, op0=ALU.mult), reads=[b_const], writes=[b_const])
        for j in range(2):
            op('dve', lambda e: e.tensor_tensor(out=lprod[:, j], in0=lqk[:, 2 * j], in1=lqk[:, 2 * j + 1], op=ALU.mult), reads=[b_const], writes=[b_const])
            op('dve', lambda e: e.tensor_reduce(out=lsum[:, j, :], in_=lprod[:, j], axis=mybir.AxisListType.X, op=ALU.add), reads=[b_const], writes=[b_const])
        op('act', lambda e: e.activation(out=lsum[:], in_=lsum[:], func=AF.Exp), reads=[b_const], writes=[b_const])
        op('dve', lambda e: e.tensor_tensor(out=nlam[:], in0=lsum[:, 1, :], in1=lsum[:, 0, :], op=ALU.subtract), reads=[b_const], writes=[b_const])
        for l in range(L):
            li = 0.8 - 0.6 * math.exp(-0.3 * l)
            op('dve', lambda e: e.tensor_scalar(out=nlam[:, l:l + 1], in0=nlam[:, l:l + 1], scalar1=-li, scalar2=None, op0=ALU.add), reads=[b_const], writes=[b_const])
            op('dve', lambda e: e.tensor_scalar(out=dng[:, l:l + 1], in0=dng[:, l:l + 1], scalar1=1.0 - li, scalar2=None, op0=ALU.mult), reads=[b_const], writes=[b_const])

    psrr = [0]

    def next_ps():
        i = psrr[0] % 8
        psrr[0] += 1
        return i

    def rmsnorm_tile(st_bufs, xt, b_xt, g_ap, tt, fin_out=None, b_fin=None):
        sq, b_sq, sd, b_sd = st_bufs
        op('act', lambda e: e.activation(out=sq[:], in_=xt[:], func=AF.Square), reads=[b_xt], writes=[b_sq])
        pi = next_ps()
        for c in range(KC):
            op('pe', lambda e: e.matmul(ps[pi][:, :], lhsT=ones_bf[:], rhs=sq[:, c, :], start=(c == 0), stop=(c == KC - 1)),
               reads=[b_sq, b_const], writes=[bps[pi]], sig=(c == KC - 1))
        op('act', lambda e: e.activation(out=sd[:], in_=ps[pi][:, :], func=AF.Ln, bias=eps_t[:], scale=1.0 / D),
           reads=[bps[pi], b_const], writes=[b_sd])
        op('act', lambda e: e.activation(out=sd[:], in_=sd[:], func=AF.Exp, scale=-0.5), reads=[b_sd], writes=[b_sd])
        for c in range(KC):
            if fin_out is None:
                op('dve', lambda e: e.scalar_tensor_tensor(out=hT[:, c, tt * 512:(tt + 1) * 512], in0=xt[:, c, :], scalar=g_ap[:, c:c + 1],
                                                           in1=sd[:], op0=ALU.mult, op1=ALU.mult),
                   reads=[b_xt, b_sd, b_const], writes=[bhT[tt]])
            else:
                op('dve', lambda e: e.scalar_tensor_tensor(out=fin_out[:, c, :], in0=xt[:, c, :], scalar=g_ap[:, c:c + 1],
                                                           in1=sd[:], op0=ALU.mult, op1=ALU.mult),
                   reads=[b_xt, b_sd, b_const], writes=[b_fin])

    bx_d = [Buf() for _ in range(NT)]

    def phase0():
        with ExitStack() as st:
            xin = [alloc(st, "p0_xin%d" % i, [128, D], F32) for i in range(2)]
            bxin = [Buf() for _ in range(2)]
            xt = [alloc(st, "p0_xt%d" % i, [128, KC, 512], F32) for i in range(2)]
            bxt = [Buf() for _ in range(2)]
            sq = alloc(st, "p0_sq", [128, KC, 512], BF16)
            sd = alloc(st, "p0_sd", [128, 512], F32)
            nb = (Buf(), Buf())
            for tt in range(NT):
                xs = xt[tt % 2]
                bxs = bxt[tt % 2]
                for j in range(4):
                    tb = tt * 4 + j
                    xi = xin[tb % 2]
                    bxi = bxin[tb % 2]
                    op('sp', lambda e: e.dma_start(out=xi[:], in_=x_in[tb * 128:(tb + 1) * 128, :]), writes=[bxi], chan='ld')
                    for half in range(2):
                        pi = next_ps()
                        for q in range(4):
                            c = half * 4 + q
                            op('pe', lambda e: e.transpose(ps[pi][:, q * 128:(q + 1) * 128], xi[:, c * 128:(c + 1) * 128], ident[:]),
                               reads=[bxi, b_const], writes=[bps[pi]], sig=(q == 3))
                        eng = 'dve' if half == 0 else 'act'
                        if eng == 'dve':
                            op('dve', lambda e: e.tensor_copy(out=xs[:, half * 4:(half + 1) * 4, j * 128:(j + 1) * 128],
                                                              in_=ps[pi][:, :].rearrange("p (a b) -> p a b", a=4)),
                               reads=[bps[pi]], writes=[bxs])
                        else:
                            op('act', lambda e: e.activation(out=xs[:, half * 4:(half + 1) * 4, j * 128:(j + 1) * 128],
                                                             in_=ps[pi][:, :].rearrange("p (a b) -> p a b", a=4), func=AF.Copy),
                               reads=[bps[pi]], writes=[bxs])
                op('sp', lambda e: e.dma_start(out=xT_d[:, tt * 512:(tt + 1) * 512].rearrange("(c p) t -> p c t", p=128), in_=xs[:]),
                   reads=[bxs], writes=[bx_d[tt]], chan='st')
                rmsnorm_tile((sq, nb[0], sd, nb[1]), xs, bxs, g_mix[:, 0, :], tt)
            fw.barrier()

    phase0()
    if dbg:
        hT_dbg = nc.dram_tensor("dbg_hT", [128, KC, S], BF16, kind="ExternalOutput").ap()
        dbg_act = nc.dram_tensor("dbg_act", [128, KC, S], BF16, kind="ExternalOutput").ap()
        dbg_y = nc.dram_tensor("dbg_y", [128, KC, S], BF16, kind="ExternalOutput").ap()
        dbg_rstd = nc.dram_tensor("dbg_rstd", [128, S], F32, kind="ExternalOutput").ap()
        dbg_mean = nc.dram_tensor("dbg_mean", [128, S], F32, kind="ExternalOutput").ap()

        def dump_hT():
            op('sp', lambda e: e.dma_start(out=hT_dbg[:, :, :], in_=hT[:]), reads=bhT, writes=[Buf()], chan='st')
            fw.barrier()
        if stop_after == 'p0':
            dump_hT()
    if stop_after == 'p0':
        fw.barrier()
        fw.close()
        top.close()
        return nc

    def fm_resources(st, pfx):
        r = {}
        r['wst'] = [alloc(st, pfx + "_wst%d" % i, [128, KC, 128], F32) for i in range(4)]
        r['bwst'] = [Buf() for _ in range(4)]
        r['wbf'] = [alloc(st, pfx + "_wbf%d" % i, [128, KC, 128], BF16) for i in range(4)]
        r['bwbf'] = [Buf() for _ in range(4)]
        r['ot'] = [alloc(st, pfx + "_ot%d" % i, [128, 512], BF16) for i in range(4)]
        r['bot'] = [Buf() for _ in range(4)]
        r['tmp'] = [alloc(st, pfx + "_tmp%d" % i, [128, 512], F32) for i in range(4)]
        r['btmp'] = [Buf() for _ in range(4)]
        r['wi'] = 0
        r['oi'] = 0
        r['ti'] = 0
        return r

    def fm_run(r, Wl, jobs):
        def load(job):
            slots = []
            for (c0, n) in job[0]:
                i = r['wi'] % 4
                r['wi'] += 1
                op('sp', lambda e: e.dma_start(out=r['wst'][i][:, :, 0:n], in_=Wl[:, c0:c0 + n].rearrange("(kc p) f -> p kc f", p=128)),
                   writes=[r['bwst'][i]], chan='wld')
                op('pool', lambda e: e.tensor_copy(out=r['wbf'][i][:, :, 0:n], in_=r['wst'][i][:, :, 0:n]),
                   reads=[r['bwst'][i]], writes=[r['bwbf'][i]])
                slots.append(i)
            return slots
        nxt = load(jobs[0]) if jobs else None
        for ji, job in enumerate(jobs):
            slots = nxt
            if ji + 1 < len(jobs):
                nxt = load(jobs[ji + 1])
            for tt in range(NT):
                pis = []
                for gi, (c0, n) in enumerate(job[0]):
                    pi = next_ps()
                    pis.append(pi)
                    wb = r['wbf'][slots[gi]]
                    for kc in range(KC):
                        op('pe', lambda e: e.matmul(ps[pi][0:n, :], lhsT=wb[:, kc, 0:n], rhs=hT[:, kc, tt * 512:(tt + 1) * 512],
                                                    start=(kc == 0), stop=(kc == KC - 1)),
                           reads=[r['bwbf'][slots[gi]], bhT[tt]], writes=[bps[pi]], sig=(kc == KC - 1))
                job[1](tt, pis)
            if job[2] is not None:
                job[2]()

    def get_ot(r):
        i = r['oi'] % 4
        r['oi'] += 1
        return r['ot'][i], r['bot'][i]

    def get_tmp(r):
        i = r['ti'] % 4
        r['ti'] += 1
        return r['tmp'][i], r['btmp'][i]

    evrr = [0]

    def evac_copy(out_ap, in_ap, reads, writes):
        evrr[0] += 1
        if evrr[0] % 2 == 0:
            op('dve', lambda e: e.tensor_copy(out=out_ap, in_=in_ap), reads=reads, writes=writes)
        else:
            op('act', lambda e: e.activation(out=out_ap, in_=in_ap, func=AF.Copy), reads=reads, writes=writes)

    def store_fm(r, dst, row0, n, tt, o, bo, bdst):
        op('sp', lambda e: e.dma_start(out=dst[row0:row0 + n, tt * 512:(tt + 1) * 512], in_=o[0:n, :]), reads=[bo], writes=[bdst], chan='st')

    bd = {k: Buf() for k in ['cv', 'gq', 'gk', 'gf', 'gg', 'dq', 'dk', 'mc', 'md', 'gv', 'dv', 'gkt', 'yg', 'yd', 'aT']}

    def phaseP(l):
        Wl = w_in[l]
        with ExitStack() as st:
            r = fm_resources(st, "pp")
            ubuf = [alloc(st, "pp_u%d" % i, [128, 32 + S], BF16) for i in range(2)]
            bub = [Buf() for _ in range(2)]
            dg = [alloc(st, "pp_dg%d" % i, [128, 31, 128], BF16) for i in range(2)]
            bdg = [Buf() for _ in range(2)]
            wdw = alloc(st, "pp_wdw", [128, KC, 31], F32)
            bwdw = Buf()
            tms = alloc(st, "pp_tms", [128, KC, 512], F32)
            btms = Buf()
            tmb = [alloc(st, "pp_tmb%d" % i, [128, KC, 512], BF16) for i in range(2)]
            btmb = [Buf() for _ in range(2)]
            tmo = [alloc(st, "pp_tmo%d" % i, [128, 512], BF16) for i in range(4)]
            btmo = [Buf() for _ in range(4)]
            with nc.allow_non_contiguous_dma(reason="depthwise conv taps, 127KB once per layer"):
                for c in range(KC):
                    op('sp', lambda e: e.dma_start(out=wdw[:, c, :], in_=conv_dw[l][:, c * 128:(c + 1) * 128].rearrange("k p -> p k")),
                       writes=[bwdw], chan='cst')
            for i in range(2):
                op('pool', lambda e: e.memset(ubuf[i][:, 0:32], 0.0), writes=[bub[i]])
            jobs = []

            def mk_conv(i):
                def evac(tt, pis):
                    t, bt = get_tmp(r)
                    op('act', lambda e: e.activation(out=t[:], in_=ps[pis[0]][:, :], func=AF.Sigmoid), reads=[bps[pis[0]]], writes=[bt])
                    op('dve', lambda e: e.tensor_tensor(out=ubuf[i % 2][:, 32 + tt * 512:32 + (tt + 1) * 512], in0=ps[pis[1]][:, :], in1=t[:], op=ALU.mult),
                       reads=[bps[pis[1]], bt], writes=[bub[i % 2]])

                def post():
                    d = dg[i % 2]
                    for k in range(31):
                        op('dve', lambda e: e.tensor_scalar(out=d[:, k, :], in0=ident_bf[:], scalar1=wdw[:, i, k:k + 1], scalar2=None, op0=ALU.mult),
                           reads=[b_const, bwdw], writes=[bdg[i % 2]])
                    for tt in range(NT):
                        pi = next_ps()
                        for k in range(31):
                            op('pe', lambda e: e.matmul(ps[pi][:, :], lhsT=d[:, k, :], rhs=ubuf[i % 2][:, 2 + k + tt * 512:2 + k + (tt + 1) * 512],
                                                        start=(k == 0), stop=(k == 30)),
                               reads=[bdg[i % 2], bub[i % 2]], writes=[bps[pi]], sig=(k == 30))
                        o, bo = get_ot(r)
                        evac_copy(o[:], ps[pi][:, :], [bps[pi]], [bo])
                        store_fm(r, cv_d, i * 128, 128, tt, o, bo, bd['cv'])
                return ([(O_CGATE + i * 128, 128), (O_CVAL + i * 128, 128)], evac, post)

            def mk_plain(c0, dst, bdst, row0, n=128, func=None):
                def evac(tt, pis):
                    o, bo = get_ot(r)
                    if func is None:
                        evac_copy(o[0:n, :], ps[pis[0]][0:n, :], [bps[pis[0]]], [bo])
                    else:
                        op('act', lambda e: e.activation(out=o[0:n, :], in_=ps[pis[0]][0:n, :], func=func), reads=[bps[pis[0]]], writes=[bo])
                    store_fm(r, dst, row0, n, tt, o, bo, bdst)
                return ([(c0, n)], evac, None)

            def mk_gg(i):
                def evac(tt, pis):
                    t1, bt1 = get_tmp(r)
                    t2, bt2 = get_tmp(r)
                    o, bo = get_ot(r)
                    op('act', lambda e: e.activation(out=t1[:], in_=ps[pis[0]][:, :], func=AF.Sigmoid), reads=[bps[pis[0]]], writes=[bt1])
                    op('act', lambda e: e.activation(out=t2[:], in_=ps[pis[1]][:, :], func=AF.Sigmoid), reads=[bps[pis[1]]], writes=[bt2])
                    op('dve', lambda e: e.tensor_tensor(out=t1[:], in0=ps[pis[0]][:, :], in1=t1[:], op=ALU.mult), reads=[bps[pis[0]], bt1], writes=[bt1])
                    op('pool', lambda e: e.tensor_tensor(out=o[:], in0=t1[:], in1=t2[:], op=ALU.mult), reads=[bt1, bt2], writes=[bo])
                    store_fm(r, gg_d, i * 128, 128, tt, o, bo, bd['gg'])
                return ([(O_GOG + i * 128, 128), (O_MG + i * 128, 128)], evac, None)

            for i in range(8):
                jobs.append(mk_conv(i))
            for i in range(4):
                jobs.append(mk_plain(O_GQ + i * 128, gq_d, bd['gq'], i * 128))
            for i in range(4):
                jobs.append(mk_plain(O_GK + i * 128, gk_d, bd['gk'], i * 128))
            jobs.append(mk_plain(O_GF, gf_d, bd['gf'], 0, n=16))
            for i in range(8):
                jobs.append(mk_gg(i))
            for i in range(8):
                jobs.append(mk_plain(O_DQ + i * 128, dq_d, bd['dq'], i * 128))
            for i in range(8):
                jobs.append(mk_plain(O_DK + i * 128, dk_d, bd['dk'], i * 128))
            for i in range(8):
                jobs.append(mk_plain(O_MC + i * 128, mc_d, bd['mc'], i * 128, func=AF.Sigmoid))
            for i in range(8):
                jobs.append(mk_plain(O_MD + i * 128, md_d, bd['md'], i * 128, func=AF.Sigmoid))
            fm_run(r, Wl, jobs)

            tmjobs = [(O_GV, gv_d, bd['gv'], 0), (O_GV + 512, gv_d, bd['gv'], 512), (O_DV, dv_d, bd['dv'], 0),
                      (O_DV + 512, dv_d, bd['dv'], 512), (O_GK, gkt_d, bd['gkt'], 0)]
            for si, (c0, dst, bdst, dc0) in enumerate(tmjobs):
                op('sp', lambda e: e.dma_start(out=tms[:], in_=Wl[:, c0:c0 + 512].rearrange("(kc p) f -> p kc f", p=128)), writes=[btms], chan='wld')
                wb = tmb[si % 2]
                op('pool', lambda e: e.tensor_copy(out=wb[:], in_=tms[:]), reads=[btms], writes=[btmb[si % 2]])
                for tb in range(NB):
                    pi = next_ps()
                    for kc in range(KC):
                        op('pe', lambda e: e.matmul(ps[pi][:, :], lhsT=hT[:, kc, tb * 128:(tb + 1) * 128], rhs=wb[:, kc, :],
                                                    start=(kc == 0), stop=(kc == KC - 1)),
                           reads=[btmb[si % 2], bhT[tb // 4]], writes=[bps[pi]], sig=(kc == KC - 1))
                    oi = (si * NB + tb) % 4
                    evac_copy(tmo[oi][:], ps[pi][:, :], [bps[pi]], [btmo[oi]])
                    op('sp', lambda e: e.dma_start(out=dst[tb * 128:(tb + 1) * 128, dc0:dc0 + 512], in_=tmo[oi][:]), reads=[btmo[oi]], writes=[bdst], chan='st')
            fw.barrier()


    def phaseG(l):
        with ExitStack() as st:
            mask01 = alloc(st, "g_mask01", [128, 4, 128], F32)
            triT = alloc(st, "g_triT", [128, 4, 128], F32)
            U = alloc(st, "g_U", [128, 128], F32)
            bfb = alloc(st, "g_bfb", [128, 512], F32)
            wfu = alloc(st, "g_wfu", [16, 512], F32)
            wfu_bf = alloc(st, "g_wfub", [16, 512], BF16)
            fT = alloc(st, "g_fT", [16, S], BF16)
            bc = Buf()
            qgb = [alloc(st, "g_qg%d" % i, [128, 4, 512], BF16) for i in range(2)]
            kgb = [alloc(st, "g_kg%d" % i, [128, 4, 512], BF16) for i in range(2)]
            kdb = [alloc(st, "g_kd%d" % i, [128, 4, 512], BF16) for i in range(2)]
            decb = [alloc(st, "g_dec%d" % i, [128, 4, 4], F32) for i in range(2)]
            bqk = [[Buf() for _ in range(2)] for _ in range(4)]
            bkd = [Buf() for _ in range(2)]
            S_f = [alloc(st, "g_Sf%d" % h, [128, 256], F32) for h in range(4)]
            S_b = [alloc(st, "g_Sb%d" % h, [128, 256], BF16) for h in range(4)]
            bS = [Buf() for _ in range(4)]
            raw = [alloc(st, "g_raw%d" % i, [128, 512], BF16) for i in range(4)]
            braw = [Buf() for _ in range(4)]
            f32t = [alloc(st, "g_f32t%d" % i, [128, 512], F32) for i in range(6)]
            bf32t = [Buf() for _ in range(6)]
            ktm = [alloc(st, "g_ktm%d" % i, [128, 4, 512], BF16) for i in range(2)]
            bktm = [Buf() for _ in range(2)]
            vt = [alloc(st, "g_vt%d" % i, [128, 4, 1024], BF16) for i in range(2)]
            bvt = [Buf() for _ in range(2)]
            att = [alloc(st, "g_att%d" % i, [128, 4, 128], BF16) for i in range(2)]
            batt = [Buf() for _ in range(2)]
            sqt = [alloc(st, "g_sq%d" % i, [128, 512], BF16) for i in range(2)]
            bsq = [Buf() for _ in range(2)]
            ggt = [alloc(st, "g_ggt%d" % i, [128, 512], BF16) for i in range(4)]
            bggt = [Buf() for _ in range(4)]
            yo = [alloc(st, "g_yo%d" % i, [128, 512], BF16) for i in range(4)]
            byo = [Buf() for _ in range(4)]
            cnt = {'f': 0, 'r': 0, 'g': 0, 'y': 0}

            def f32():
                i = cnt['f'] % 6
                cnt['f'] += 1
                return f32t[i], bf32t[i]

            def rawt():
                i = cnt['r'] % 4
                cnt['r'] += 1
                return raw[i], braw[i]

            op('pool', lambda e: e.memset(mask01[:], 1.0), writes=[bc])
            op('pool', lambda e: e.memset(mask01[:, :, 0:1], 0.0), writes=[bc])
            op('pool', lambda e: e.memset(triT[:], 1.0), writes=[bc])
            op('pool', lambda e: e.affine_select(out=triT[:], in_=triT[:], pattern=[[0, 4], [1, 128]], base=0, channel_multiplier=-1,
                                                 compare_op=ALU.is_ge, fill=0.0), reads=[bc], writes=[bc])
            op('pool', lambda e: e.memset(U[:], 1.0), writes=[bc])
            op('pool', lambda e: e.affine_select(out=U[:], in_=U[:], pattern=[[-1, 128]], base=0, channel_multiplier=1,
                                                 compare_op=ALU.is_gt, fill=0.0), reads=[bc], writes=[bc])
            with nc.allow_non_contiguous_dma(reason="bias broadcast"):
                op('sp', lambda e: e.dma_start(out=bfb[:], in_=gla_bf[l:l + 1, :].broadcast_to([128, 512])), writes=[bc], chan='cst')
            op('sp', lambda e: e.dma_start(out=wfu[:], in_=gla_wf_up[l]), writes=[bc], chan='cst')
            op('dve', lambda e: e.tensor_copy(out=wfu_bf[:], in_=wfu[:]), reads=[bc], writes=[bc])
            op('sp', lambda e: e.dma_start(out=fT[:], in_=gf_d[:, :]), reads=[bd['gf']], writes=[bc], chan='ld')
            for h in range(4):
                op('pool', lambda e: e.memset(S_f[h][:], 0.0), writes=[bS[h]])
                op('pool', lambda e: e.memset(S_b[h][:], 0.0), writes=[bS[h]])

            def prep(tt):
                tsl = slice(tt * 512, (tt + 1) * 512)
                qg, kg, kd, dec = qgb[tt % 2], kgb[tt % 2], kdb[tt % 2], decb[tt % 2]
                for h in range(4):
                    pi = next_ps()
                    op('pe', lambda e: e.matmul(ps[pi][:, :], lhsT=wfu_bf[0:16, h * 128:(h + 1) * 128], rhs=fT[0:16, tsl], start=True, stop=True),
                       reads=[bc], writes=[bps[pi]])
                    E, bE = f32()
                    op('act', lambda e: e.activation(out=E[:], in_=ps[pi][:, :], func=AF.Exp, bias=nbf[:, l, h:h + 1], scale=-1.0),
                       reads=[bps[pi], b_const], writes=[bE])
                    op('act', lambda e: e.activation(out=E[:], in_=E[:], func=AF.Ln, bias=one_t[:], scale=1.0), reads=[bE, b_const], writes=[bE])
                    Gp, bG = f32()
                    op('dve', lambda e: e.tensor_tensor_scan(out=Gp[:], data0=mask01[:].rearrange("p a b -> p (a b)"), data1=E[:], initial=0.0,
                                                             op0=ALU.mult, op1=ALU.add), reads=[bE, bc], writes=[bG])
                    EG, bEG = f32()
                    op('act', lambda e: e.activation(out=EG[:], in_=Gp[:], func=AF.Exp, bias=lnsc_t[:], scale=-1.0 / 16), reads=[bG, b_const], writes=[bEG])
                    qr, bqr = rawt()
                    op('sp', lambda e: e.dma_start(out=qr[:], in_=gq_d[h * 128:(h + 1) * 128, tsl]), reads=[bd['gq']], writes=[bqr], chan='ld')
                    op('dve', lambda e: e.tensor_tensor(out=qg[:, h, :], in0=qr[:], in1=EG[:], op=ALU.mult), reads=[bqr, bEG], writes=[bqk[h][tt % 2]])
                    EN, bEN = f32()
                    op('act', lambda e: e.activation(out=EN[:], in_=Gp[:], func=AF.Exp, scale=1.0 / 16), reads=[bG], writes=[bEN])
                    kr, bkr = rawt()
                    op('sp', lambda e: e.dma_start(out=kr[:], in_=gk_d[h * 128:(h + 1) * 128, tsl]), reads=[bd['gk']], writes=[bkr], chan='ld')
                    op('dve', lambda e: e.tensor_tensor(out=kg[:, h, :], in0=kr[:], in1=EN[:], op=ALU.mult), reads=[bkr, bEN], writes=[bqk[h][tt % 2]])
                    op('act', lambda e: e.activation(out=dec[:, h, :], in_=Gp[:].rearrange("p (b t) -> p b t", t=128)[:, :, 127],
                                                     func=AF.Exp, scale=-1.0 / 16), reads=[bG], writes=[bqk[h][tt % 2]])
                kt = ktm[tt % 2]
                op('sp', lambda e: e.dma_start(out=kt[:], in_=gkt_d[tsl, :].rearrange("(b p) f -> p b f", p=128)), reads=[bd['gkt']], writes=[bktm[tt % 2]], chan='ld')
                for blk in range(4):
                    tb = tt * 4 + blk
                    pi = next_ps()
                    op('pe', lambda e: e.matmul(ps[pi][:, :], lhsT=fT[0:16, tb * 128:(tb + 1) * 128], rhs=wfu_bf[0:16, :], start=True, stop=True),
                       reads=[bc], writes=[bps[pi]])
                    Z, bZ = f32()
                    op('dve', lambda e: e.tensor_tensor(out=Z[:], in0=ps[pi][:, :], in1=bfb[:], op=ALU.add), reads=[bps[pi], bc], writes=[bZ])
                    op('act', lambda e: e.activation(out=Z[:], in_=Z[:], func=AF.Exp, scale=-1.0), reads=[bZ], writes=[bZ])
                    op('act', lambda e: e.activation(out=Z[:], in_=Z[:], func=AF.Ln, bias=one_t[:], scale=1.0), reads=[bZ, b_const], writes=[bZ])
                    pi2 = next_ps()
                    op('pe', lambda e: e.matmul(ps[pi2][:, :], lhsT=U[:], rhs=Z[:], start=True, stop=True), reads=[bc, bZ], writes=[bps[pi2]])
                    ED, bED = f32()
                    op('act', lambda e: e.activation(out=ED[:], in_=ps[pi2][:, :], func=AF.Exp, scale=-1.0 / 16), reads=[bps[pi2]], writes=[bED])
                    op('dve', lambda e: e.tensor_tensor(out=kd[:, blk, :], in0=kt[:, blk, :], in1=ED[:], op=ALU.mult), reads=[bktm[tt % 2], bED], writes=[bkd[tt % 2]])

            def recur(tt):
                tsl = slice(tt * 512, (tt + 1) * 512)
                qg, kg, kd, dec = qgb[tt % 2], kgb[tt % 2], kdb[tt % 2], decb[tt % 2]
                v = vt[tt % 2]
                op('sp', lambda e: e.dma_start(out=v[:], in_=gv_d[tsl, :].rearrange("(b p) f -> p b f", p=128)), reads=[bd['gv']], writes=[bvt[tt % 2]], chan='ld')
                for pair in range(2):
                    heads = (2 * pair, 2 * pair + 1)
                    for hi, h in enumerate(heads):
                        a = att[hi]
                        for blk in range(4):
                            bsl = slice(blk * 128, (blk + 1) * 128)
                            op('pe', lambda e: e.matmul(ps[4][:, blk * 128:(blk + 1) * 128], lhsT=kg[:, h, bsl], rhs=qg[:, h, bsl], start=True, stop=True),
                               reads=[bqk[h][tt % 2]], writes=[bps[4]], sig=(blk == 3))
                        op('dve', lambda e: e.tensor_tensor(out=a[:], in0=ps[4][:, :].rearrange("p (a b) -> p a b", a=4), in1=triT[:], op=ALU.mult),
                           reads=[bps[4], bc], writes=[batt[hi]])
                    for blk in range(4):
                        tb = tt * 4 + blk
                        bsl = slice(blk * 128, (blk + 1) * 128)
                        for hi, h in enumerate(heads):
                            for vh in range(2):
                                pb = hi * 2 + vh
                                op('pe', lambda e: e.matmul(ps[pb][:, blk * 128:(blk + 1) * 128], lhsT=v[:, blk, h * 256 + vh * 128:h * 256 + (vh + 1) * 128],
                                                            rhs=att[hi][:, blk, :], start=True, stop=False),
                                   reads=[bvt[tt % 2], batt[hi]], writes=[bps[pb]], sig=False)
                                op('pe', lambda e: e.matmul(ps[pb][:, blk * 128:(blk + 1) * 128], lhsT=S_b[h][:, vh * 128:(vh + 1) * 128],
                                                            rhs=qg[:, h, bsl], start=False, stop=True),
                                   reads=[bS[h], bqk[h][tt % 2]], writes=[bps[pb]], sig=True)
                            pd = 5 + hi
                            op('pe', lambda e: e.matmul(ps[pd][:, 0:256], lhsT=kd[:, blk, h * 128:(h + 1) * 128], rhs=v[:, blk, h * 256:(h + 1) * 256], start=True, stop=True),
                               reads=[bkd[tt % 2], bvt[tt % 2]], writes=[bps[pd]])
                            op('dve', lambda e: e.scalar_tensor_tensor(out=S_f[h][:], in0=S_f[h][:], scalar=dec[:, h, blk:blk + 1], in1=ps[pd][:, 0:256],
                                                                       op0=ALU.mult, op1=ALU.add),
                               reads=[bS[h], bps[pd], bqk[h][tt % 2]], writes=[bS[h]])
                            op('dve', lambda e: e.tensor_copy(out=S_b[h][:], in_=S_f[h][:]), reads=[bS[h]], writes=[bS[h]])
                    for hi, h in enumerate(heads):
                        for vh in range(2):
                            pb = hi * 2 + vh
                            op('act', lambda e: e.activation(out=sqt[vh][:], in_=ps[pb][:, :], func=AF.Square), reads=[bps[pb]], writes=[bsq[vh]])
                            op('pe', lambda e: e.matmul(ps[7][:, :], lhsT=ones_bf[:], rhs=sqt[vh][:], start=(vh == 0), stop=(vh == 1)),
                               reads=[bsq[vh], b_const], writes=[bps[7]], sig=(vh == 1))
                        sd, bsd = f32()
                        op('act', lambda e: e.activation(out=sd[:], in_=ps[7][:, :], func=AF.Ln, bias=eps_t[:], scale=1.0 / 256), reads=[bps[7], b_const], writes=[bsd])
                        op('act', lambda e: e.activation(out=sd[:], in_=sd[:], func=AF.Exp, scale=-0.5), reads=[bsd], writes=[bsd])
                        for vh in range(2):
                            pb = hi * 2 + vh
                            gi = cnt['g'] % 4
                            cnt['g'] += 1
                            r0 = h * 256 + vh * 128
                            op('sp', lambda e: e.dma_start(out=ggt[gi][:], in_=gg_d[r0:r0 + 128, tsl]), reads=[bd['gg']], writes=[bggt[gi]], chan='ld')
                            y, by = f32()
                            op('dve', lambda e: e.scalar_tensor_tensor(out=y[:], in0=ps[pb][:, :], scalar=gng[:, l, vh:vh + 1], in1=sd[:], op0=ALU.mult, op1=ALU.mult),
                               reads=[bps[pb], bsd, b_const], writes=[by])
                            yi = cnt['y'] % 4
                            cnt['y'] += 1
                            op('pool', lambda e: e.tensor_tensor(out=yo[yi][:], in0=y[:], in1=ggt[gi][:], op=ALU.mult), reads=[by, bggt[gi]], writes=[byo[yi]])
                            op('sp', lambda e: e.dma_start(out=yg_d[r0:r0 + 128, tsl], in_=yo[yi][:]), reads=[byo[yi]], writes=[bd['yg']], chan='st')

            prep(0)
            for tt in range(NT):
                if tt + 1 < NT:
                    prep(tt + 1)
                recur(tt)
            fw.barrier()


    def phaseD(l):
        with ExitStack() as st:
            QT = [alloc(st, "d_QT%d" % i, [128, 2, S], BF16) for i in range(2)]
            KT = [alloc(st, "d_KT%d" % i, [128, S], BF16) for i in range(2)]
            V = [alloc(st, "d_V%d" % i, [128, NB, 128], BF16) for i in range(2)]
            bqkv = [Buf() for _ in range(2)]
            NP = 4
            pt = [alloc(st, "d_pt%d" % i, [128, 2, 256], BF16) for i in range(NP)]
            bpt = [Buf() for _ in range(NP)]
            f32t = [alloc(st, "d_f32t%d" % i, [128, 512], F32) for i in range(9)]
            bf32t = [Buf() for _ in range(9)]
            sqt = [alloc(st, "d_sq%d" % i, [128, 256], BF16) for i in range(2)]
            bsq = [Buf() for _ in range(2)]
            mdt = [alloc(st, "d_md%d" % i, [128, 256], BF16) for i in range(2)]
            bmdt = [Buf() for _ in range(2)]
            yo = [alloc(st, "d_yo%d" % i, [128, 256], BF16) for i in range(2)]
            byo = [Buf() for _ in range(2)]
            bzero = Buf()
            for i in range(2):
                op('pool', lambda e: e.memset(QT[i][64:128, 0, :], 0.0), writes=[bqkv[i]])
                op('pool', lambda e: e.memset(QT[i][0:64, 1, :], 0.0), writes=[bqkv[i]])
            cnt = {'f': 0}

            def f32():
                i = cnt['f'] % 9
                cnt['f'] += 1
                return f32t[i], bf32t[i]

            NQ = S // 256
            items = []
            for h in range(8):
                for j in range(NQ):
                    for kb in range(2 * j + 2):
                        items.append((h, j, kb))

            def load_head(h):
                hb = h % 2
                op('sp', lambda e: e.dma_start(out=QT[hb][0:64, 0, :], in_=dq_d[h * 128:h * 128 + 64, :]), reads=[bd['dq']], writes=[bqkv[hb]], chan='ld')
                op('sp', lambda e: e.dma_start(out=QT[hb][64:128, 1, :], in_=dq_d[h * 128 + 64:(h + 1) * 128, :]), reads=[bd['dq']], writes=[bqkv[hb]], chan='ld')
                op('sp', lambda e: e.dma_start(out=KT[hb][:], in_=dk_d[h * 128:(h + 1) * 128, :]), reads=[bd['dk']], writes=[bqkv[hb]], chan='ld')
                op('sp', lambda e: e.dma_start(out=V[hb][:], in_=dv_d[:, h * 128:(h + 1) * 128].rearrange("(b p) f -> p b f", p=128)),
                   reads=[bd['dv']], writes=[bqkv[hb]], chan='ld')

            def c0_of(j, kb):
                return 128 * (kb - 2 * j) if kb >= 2 * j else 0

            def do_S(i):
                h, j, kb = items[i]
                hb = h % 2
                c0 = c0_of(j, kb)
                q0 = j * 256
                sb = i % 4
                for m in range(2):
                    op('pe', lambda e: e.matmul(ps[sb][:, m * 256 + c0:(m + 1) * 256], lhsT=KT[hb][:, kb * 128:(kb + 1) * 128],
                                                rhs=QT[hb][:, m, q0 + c0:q0 + 256], start=True, stop=True),
                       reads=[bqkv[hb]], writes=[bps[sb]], sig=(m == 1))
                pi = i % NP
                op('act', lambda e: e.activation(out=pt[pi][:, :, c0:256], in_=ps[sb].rearrange("p (m q) -> p m q", m=2)[:, :, c0:256], func=AF.Exp, scale=0.125),
                   reads=[bps[sb]], writes=[bpt[pi]])
                if kb >= 2 * j:
                    op('pool', lambda e: e.memset(pt[pi][64:128, :, c0:c0 + 64], 0.0), reads=[], writes=[bpt[pi]])

            def do_PV(i):
                h, j, kb = items[i]
                hb = h % 2
                c0 = c0_of(j, kb)
                pi = i % NP
                par = (h * NQ + j) % 2
                ob, db = 4 + par, 6 + par
                first = (kb == 0)
                last = (kb == 2 * j + 1)
                if c0 == 0:
                    op('pe', lambda e: e.matmul(ps[ob][:, :], lhsT=V[hb][:, kb, :], rhs=pt[pi][:].rearrange("p m q -> p (m q)"), start=first, stop=last),
                       reads=[bqkv[hb], bpt[pi]], writes=[bps[ob]], sig=last)
                    op('pe', lambda e: e.matmul(ps[db][:, :], lhsT=ones_bf[:], rhs=pt[pi][:].rearrange("p m q -> p (m q)"), start=first, stop=last),
                       reads=[b_const, bpt[pi]], writes=[bps[db]], sig=True)
                else:
                    for m in range(2):
                        op('pe', lambda e: e.matmul(ps[ob][:, m * 256 + c0:(m + 1) * 256], lhsT=V[hb][:, kb, :], rhs=pt[pi][:, m, c0:256], start=False, stop=(last and m == 1)),
                           reads=[bqkv[hb], bpt[pi]], writes=[bps[ob]], sig=(last and m == 1))
                        op('pe', lambda e: e.matmul(ps[db][:, m * 256 + c0:(m + 1) * 256], lhsT=ones_bf[:], rhs=pt[pi][:, m, c0:256], start=False, stop=(last and m == 1)),
                           reads=[b_const, bpt[pi]], writes=[bps[db]], sig=(m == 1))

            epi = {}

            def epilogueA(h, j):
                par = (h * NQ + j) % 2
                ob, db = 4 + par, 6 + par
                q0 = j * 256
                op('sp', lambda e: e.dma_start(out=mdt[par][:], in_=md_d[h * 128:(h + 1) * 128, q0:q0 + 256]), reads=[bd['md']], writes=[bmdt[par]], chan='ld')
                r, br = f32()
                op('dve', lambda e: e.reciprocal(out=r[:], in_=ps[db][:, :]), reads=[bps[db]], writes=[br])
                op('dve', lambda e: e.tensor_tensor(out=r[:], in0=ps[ob][:, :], in1=r[:], op=ALU.mult), reads=[bps[ob], br], writes=[br])
                O, bO = f32()
                op('dve', lambda e: e.scalar_tensor_tensor(out=O[:, 0:256], in0=r[:, 256:512], scalar=nlam[:, l:l + 1], in1=r[:, 0:256], op0=ALU.mult, op1=ALU.add),
                   reads=[br, b_const], writes=[bO])
                op('dve', lambda e: e.tensor_tensor(out=sqt[par][:], in0=O[:, 0:256], in1=O[:, 0:256], op=ALU.mult), reads=[bO], writes=[bsq[par]])
                epi[(h, j)] = (O, bO)

            def epilogueB(h, j):
                par = (h * NQ + j) % 2
                q0 = j * 256
                O, bO = epi.pop((h, j))
                pn = next_ps() % 4
                pn_b = [bps[pn]]
                op('pe', lambda e: e.matmul(ps[pn][:, 0:256], lhsT=ones_bf[:], rhs=sqt[par][:], start=True, stop=True), reads=[b_const, bsq[par]], writes=pn_b)
                sd, bsd = f32()
                op('act', lambda e: e.activation(out=sd[:, 0:256], in_=ps[pn][:, 0:256], func=AF.Ln, bias=eps_t[:], scale=1.0 / 128), reads=pn_b + [b_const], writes=[bsd])
                op('act', lambda e: e.activation(out=sd[:, 0:256], in_=sd[:, 0:256], func=AF.Exp, scale=-0.5), reads=[bsd], writes=[bsd])
                op('dve', lambda e: e.scalar_tensor_tensor(out=O[:, 0:256], in0=O[:, 0:256], scalar=dng[:, l:l + 1], in1=sd[:, 0:256], op0=ALU.mult, op1=ALU.mult),
                   reads=[bO, bsd, b_const], writes=[bO])
                op('pool', lambda e: e.tensor_tensor(out=yo[par][:], in0=O[:, 0:256], in1=mdt[par][:], op=ALU.mult), reads=[bO, bmdt[par]], writes=[byo[par]])
                op('sp', lambda e: e.dma_start(out=yd_d[h * 128:(h + 1) * 128, q0:q0 + 256], in_=yo[par][:]), reads=[byo[par]], writes=[bd['yd']], chan='st')

            import os
            if os.environ.get('DBG_D_ITEMS'):
                items[:] = items[:int(os.environ['DBG_D_ITEMS'])]
            load_head(0)
            LOOK = 2
            pendB = []
            for i in range(min(LOOK, len(items))):
                do_S(i)
            for i in range(len(items)):
                h, j, kb = items[i]
                if j == 0 and kb == 0 and h + 1 < 8:
                    load_head(h + 1)
                if i + LOOK < len(items):
                    do_S(i + LOOK)
                do_PV(i)
                if pendB and pendB[0][0] <= i:
                    _, hh, jj = pendB.pop(0)
                    epilogueB(hh, jj)
                if kb == 2 * j + 1:
                    epilogueA(h, j)
                    pendB.append((i + 3, h, j))
            for _, hh, jj in pendB:
                epilogueB(hh, jj)
            fw.barrier()

    def load_resident(st, pfx, Wl, K, ncols, dst, bdst, stage, bstage):
        kcn = K // 128
        for j in range(ncols // 128):
            sidx = j % len(stage)
            op('sp', lambda e: e.dma_start(out=stage[sidx][:, 0:kcn, :], in_=Wl[:, j * 128:(j + 1) * 128].rearrange("(kc p) f -> p kc f", p=128)),
               writes=[bstage[sidx]], chan='wld')
            op('pool', lambda e: e.tensor_copy(out=dst[:, :, j * 128:(j + 1) * 128], in_=stage[sidx][:, 0:kcn, :]), reads=[bstage[sidx]], writes=[bdst])

    def phaseM(l):
        with ExitStack() as st:
            stage = [alloc(st, "m_stg%d" % i, [128, KC, 128], F32) for i in range(2)]
            bstage = [Buf() for _ in range(2)]
            wpw = alloc(st, "m_wpw", [128, KC, D], BF16)
            wo = alloc(st, "m_wo", [128, KC, D], BF16)
            bwpw, bwo = Buf(), Buf()
            load_resident(st, "m", w_conv_out[l], D, D, wpw, bwpw, stage, bstage)
            load_resident(st, "m", w_out[l], D, D, wo, bwo, stage, bstage)
            cvt = [alloc(st, "m_cvt%d" % i, [128, KC, 512], BF16) for i in range(1)] * 2
            bcvt = [Buf()] * 2
            mct = alloc(st, "m_mct", [128, KC, 512], BF16)
            ygt = alloc(st, "m_ygt", [128, KC, 512], BF16)
            ydt = alloc(st, "m_ydt", [128, KC, 512], BF16)
            bmct, bygt, bydt = Buf(), Buf(), Buf()
            xt = alloc(st, "m_xt", [128, KC, 512], F32)
            bxt = Buf()
            sqc = alloc(st, "m_sqc", [128, KC, 512], BF16)
            bsqc = Buf()
            actt = alloc(st, "m_act", [128, KC, 512], BF16)
            bact = Buf()
            yt = alloc(st, "m_yt", [128, KC, 512], BF16)
            byt = Buf()
            f32t = [alloc(st, "m_f32t%d" % i, [128, 512], F32) for i in range(8)]
            bf32t = [Buf() for _ in range(8)]
            sd = alloc(st, "m_sd", [128, 512], F32)
            bsd = Buf()
            cnt = {'f': 0}

            def f32():
                i = 4 + cnt['f'] % 4
                cnt['f'] += 1
                return f32t[i], bf32t[i]

            def fm_tile(dst, tt):
                return dst[:, tt * 512:(tt + 1) * 512].rearrange("(c p) t -> p c t", p=128)

            for tt in range(NT):
                cv = cvt[tt % 2]
                op('sp', lambda e: e.dma_start(out=cv[:], in_=fm_tile(cv_d, tt)), reads=[bd['cv']], writes=[bcvt[tt % 2]], chan='ld')
                op('sp', lambda e: e.dma_start(out=mct[:], in_=fm_tile(mc_d, tt)), reads=[bd['mc']], writes=[bmct], chan='ld')
                op('sp', lambda e: e.dma_start(out=ygt[:], in_=fm_tile(yg_d, tt)), reads=[bd['yg']], writes=[bygt], chan='ld')
                op('sp', lambda e: e.dma_start(out=ydt[:], in_=fm_tile(yd_d, tt)), reads=[bd['yd']], writes=[bydt], chan='ld')
                op('sp', lambda e: e.dma_start(out=xt[:], in_=fm_tile(xT_d, tt)), reads=[bx_d[tt]], writes=[bxt], chan='ld')
                op('act', lambda e: e.activation(out=sqc[:], in_=cv[:], func=AF.Square), reads=[bcvt[tt % 2]], writes=[bsqc])
                p1 = next_ps()
                for c in range(KC):
                    op('pe', lambda e: e.matmul(ps[p1][:, :], lhsT=ones_bf[:], rhs=cv[:, c, :], start=(c == 0), stop=(c == KC - 1)),
                       reads=[bcvt[tt % 2], b_const], writes=[bps[p1]], sig=(c == KC - 1))
                p2 = next_ps()
                for c in range(KC):
                    op('pe', lambda e: e.matmul(ps[p2][:, :], lhsT=ones_bf[:], rhs=sqc[:, c, :], start=(c == 0), stop=(c == KC - 1)),
                       reads=[bsqc, b_const], writes=[bps[p2]], sig=(c == KC - 1))
                mean, bmean = f32t[0], bf32t[0]
                op('act', lambda e: e.activation(out=mean[:], in_=ps[p1][:, :], func=AF.Copy, scale=1.0 / D), reads=[bps[p1]], writes=[bmean])
                msq, bmsq = f32t[1], bf32t[1]
                op('dve', lambda e: e.tensor_tensor(out=msq[:], in0=mean[:], in1=mean[:], op=ALU.mult), reads=[bmean], writes=[bmsq])
                var, bvar = f32t[2], bf32t[2]
                op('dve', lambda e: e.scalar_tensor_tensor(out=var[:], in0=ps[p2][:, :], scalar=1.0 / D, in1=msq[:], op0=ALU.mult, op1=ALU.subtract),
                   reads=[bps[p2], bmsq], writes=[bvar])
                op('dve', lambda e: e.tensor_scalar(out=var[:], in0=var[:], scalar1=0.0, scalar2=None, op0=ALU.max), reads=[bvar], writes=[bvar])
                op('act', lambda e: e.activation(out=var[:], in_=var[:], func=AF.Ln, bias=eps_t[:], scale=1.0), reads=[bvar, b_const], writes=[bvar])
                op('act', lambda e: e.activation(out=var[:], in_=var[:], func=AF.Exp, scale=-0.5), reads=[bvar], writes=[bvar])
                nb_, bnb = f32t[3], bf32t[3]
                op('dve', lambda e: e.scalar_tensor_tensor(out=nb_[:], in0=mean[:], scalar=-1.0, in1=var[:], op0=ALU.mult, op1=ALU.mult),
                   reads=[bmean, bvar], writes=[bnb])
                for c in range(KC):
                    t1, bt1 = f32()
                    op('pool', lambda e: e.tensor_tensor(out=t1[:], in0=cv[:, c, :], in1=var[:], op=ALU.mult), reads=[bcvt[tt % 2], bvar], writes=[bt1])
                    op('dve', lambda e: e.tensor_tensor(out=t1[:], in0=t1[:], in1=nb_[:], op=ALU.add), reads=[bt1, bnb], writes=[bt1])
                    op('act', lambda e: e.activation(out=actt[:, c, :], in_=t1[:], func=AF.Silu, bias=cln_b[:, l, c:c + 1], scale=cln_g[:, l, c:c + 1]),
                       reads=[bt1, b_const], writes=[bact])
                op('pool', lambda e: e.tensor_tensor(out=ygt[:], in0=ygt[:], in1=ydt[:], op=ALU.add), reads=[bygt, bydt], writes=[bygt])
                for fc in range(KC):
                    pi = next_ps()
                    for kc in range(KC):
                        op('pe', lambda e: e.matmul(ps[pi][:, :], lhsT=wpw[:, kc, fc * 128:(fc + 1) * 128], rhs=actt[:, kc, :], start=(kc == 0), stop=(kc == KC - 1)),
                           reads=[bwpw, bact], writes=[bps[pi]], sig=(kc == KC - 1))
                    t1, bt1 = f32()
                    op('dve', lambda e: e.tensor_tensor(out=t1[:], in0=ps[pi][:, :], in1=mct[:, fc, :], op=ALU.mult), reads=[bps[pi], bmct], writes=[bt1])
                    op('dve', lambda e: e.tensor_tensor(out=yt[:, fc, :], in0=t1[:], in1=ygt[:, fc, :], op=ALU.add), reads=[bt1, bygt], writes=[byt])
                for fc in range(KC):
                    pi = next_ps()
                    for kc in range(KC):
                        op('pe', lambda e: e.matmul(ps[pi][:, :], lhsT=wo[:, kc, fc * 128:(fc + 1) * 128], rhs=yt[:, kc, :], start=(kc == 0), stop=(kc == KC - 1)),
                           reads=[bwo, byt], writes=[bps[pi]], sig=(kc == KC - 1))
                    op('dve', lambda e: e.tensor_tensor(out=xt[:, fc, :], in0=ps[pi][:, :], in1=xt[:, fc, :], op=ALU.add), reads=[bps[pi], bxt], writes=[bxt])
                if dbg:
                    op('sp', lambda e: e.dma_start(out=dbg_act[:, :, tt * 512:(tt + 1) * 512], in_=actt[:]), reads=[bact], writes=[Buf()], chan='st')
                    op('sp', lambda e: e.dma_start(out=dbg_y[:, :, tt * 512:(tt + 1) * 512], in_=yt[:]), reads=[byt], writes=[Buf()], chan='st')
                    op('sp', lambda e: e.dma_start(out=dbg_rstd[:, tt * 512:(tt + 1) * 512], in_=var[:]), reads=[bvar], writes=[Buf()], chan='st')
                    op('sp', lambda e: e.dma_start(out=dbg_mean[:, tt * 512:(tt + 1) * 512], in_=mean[:]), reads=[bmean], writes=[Buf()], chan='st')
                op('sp', lambda e: e.dma_start(out=fm_tile(xT_d, tt), in_=xt[:]), reads=[bxt], writes=[bx_d[tt]], chan='st')
                rmsnorm_tile((sqc, bsqc, sd, bsd), xt, bxt, g_ffn[:, l, :], tt)
            fw.barrier()

    def phaseF1(l):
        with ExitStack() as st:
            r = fm_resources(st, "f1")

            def mk(j):
                def evac(tt, pis):
                    t, bt = get_tmp(r)
                    o, bo = get_ot(r)
                    op('act', lambda e: e.activation(out=t[:], in_=ps[pis[0]][:, :], func=AF.Silu), reads=[bps[pis[0]]], writes=[bt])
                    op('dve', lambda e: e.tensor_tensor(out=o[:], in0=ps[pis[1]][:, :], in1=t[:], op=ALU.mult), reads=[bps[pis[1]], bt], writes=[bo])
                    store_fm(r, aT_d, j * 128, 128, tt, o, bo, bd['aT'])
                return ([(j * 128, 128), (FH + j * 128, 128)], evac, None)
            fm_run(r, w_ffn_in[l], [mk(j) for j in range(FH // 128)])
            fw.barrier()

    def phaseF2(l):
        KF = FH // 128
        last = (l == L - 1)
        with ExitStack() as st:
            stage = [alloc(st, "f2_stg%d" % i, [128, KF, 128], F32) for i in range(1)]
            bstage = [Buf() for _ in range(1)]
            wfo = alloc(st, "f2_wfo", [128, KF, D], BF16)
            bwfo = Buf()
            load_resident(st, "f2", w_ffn_out[l], FH, D, wfo, bwfo, stage, bstage)
            at = [alloc(st, "f2_at%d" % i, [128, KF, 512], BF16) for i in range(1)] * 2
            bat = [Buf()] * 2
            xt = [alloc(st, "f2_xt%d" % i, [128, KC, 512], F32) for i in range(1)] * 2
            bxt = [Buf()] * 2
            sq = alloc(st, "f2_sq", [128, KC, 512], BF16)
            sd = alloc(st, "f2_sd", [128, 512], F32)
            bsq, bsd = Buf(), Buf()
            if last:
                fin = alloc(st, "f2_fin", [128, KC, 512], F32)
                bfin = Buf()
                otm = [alloc(st, "f2_otm%d" % i, [128, D], F32) for i in range(2)]
                botm = [Buf() for _ in range(2)]
            b_out = Buf()
            for tt in range(NT):
                a = at[tt % 2]
                x_ = xt[tt % 2]
                op('sp', lambda e: e.dma_start(out=a[:], in_=aT_d[:, tt * 512:(tt + 1) * 512].rearrange("(c p) t -> p c t", p=128)),
                   reads=[bd['aT']], writes=[bat[tt % 2]], chan='ld')
                op('sp', lambda e: e.dma_start(out=x_[:], in_=xT_d[:, tt * 512:(tt + 1) * 512].rearrange("(c p) t -> p c t", p=128)),
                   reads=[bx_d[tt]], writes=[bxt[tt % 2]], chan='ld')
                for fc in range(KC):
                    pi = next_ps()
                    for kc in range(KF):
                        op('pe', lambda e: e.matmul(ps[pi][:, :], lhsT=wfo[:, kc, fc * 128:(fc + 1) * 128], rhs=a[:, kc, :], start=(kc == 0), stop=(kc == KF - 1)),
                           reads=[bwfo, bat[tt % 2]], writes=[bps[pi]], sig=(kc == KF - 1))
                    op('dve', lambda e: e.tensor_tensor(out=x_[:, fc, :], in0=ps[pi][:, :], in1=x_[:, fc, :], op=ALU.add), reads=[bps[pi], bxt[tt % 2]], writes=[bxt[tt % 2]])
                if not last:
                    op('sp', lambda e: e.dma_start(out=xT_d[:, tt * 512:(tt + 1) * 512].rearrange("(c p) t -> p c t", p=128), in_=x_[:]),
                       reads=[bxt[tt % 2]], writes=[bx_d[tt]], chan='st')
                    rmsnorm_tile((sq, bsq, sd, bsd), x_, bxt[tt % 2], g_mix[:, l + 1, :], tt)
                else:
                    if dbg:
                        op('sp', lambda e: e.dma_start(out=xT_d[:, tt * 512:(tt + 1) * 512].rearrange("(c p) t -> p c t", p=128), in_=x_[:]),
                           reads=[bxt[tt % 2]], writes=[bx_d[tt]], chan='st')
                    rmsnorm_tile((sq, bsq, sd, bsd), x_, bxt[tt % 2], g_fin, tt, fin_out=fin, b_fin=bfin)
                    for j in range(4):
                        tb = tt * 4 + j
                        o_ = otm[tb % 2]
                        for half in range(2):
                            pi = next_ps()
                            for q in range(4):
                                c = half * 4 + q
                                op('pe', lambda e: e.transpose(ps[pi][:, q * 128:(q + 1) * 128], fin[:, c, j * 128:(j + 1) * 128], ident[:]),
                                   reads=[bfin, b_const], writes=[bps[pi]], sig=(q == 3))
                            evac_copy(o_[:, half * 512:(half + 1) * 512], ps[pi][:, :], [bps[pi]], [botm[tb % 2]])
                        op('sp', lambda e: e.dma_start(out=out[tb * 128:(tb + 1) * 128, :], in_=o_[:]), reads=[botm[tb % 2]], writes=[b_out], chan='out')
            fw.barrier()

    def finish():
        fw.barrier()
        fw.close()
        top.close()
        return nc

    import os as _os
    _skip = _os.environ.get('DBG_SKIP', '')
    for l in range(L):
        if 'P' not in _skip:
            phaseP(l)
        if stop_after == 'P':
            return finish()
        if 'G' not in _skip:
            phaseG(l)
        if stop_after == 'G':
            return finish()
        phaseD(l)
        if stop_after == 'D':
            return finish()
        phaseM(l)
        if stop_after == 'M':
            if dbg:
                dump_hT()
            return finish()
        phaseF1(l)
        phaseF2(l)
        if stop_after == 'F' and l == 0:
            return finish()
    return finish()


_NC_CACHE = {}


def kernel(**inputs):
    S, L, B = 4096, 4, 8
    if 'nc' not in _NC_CACHE:
        _NC_CACHE['nc'] = build_program(S=S, L=L)
    nc = _NC_CACHE['nc']
    x = np.ascontiguousarray(np.asarray(inputs['x'], dtype=np.float32))
    shared = {k: np.ascontiguousarray(np.asarray(v, dtype=np.float32)) for k, v in inputs.items() if k != 'x'}
    in_maps = []
    for b in range(B):
        m = dict(shared)
        m['x'] = x[b]
        in_maps.append(m)
    res = run_bass_kernel_spmd(nc, in_maps, core_ids=list(range(B)))
    return np.stack([np.asarray(res.results[b]['out'], dtype=np.float32) for b in range(B)], axis=0)
```
